# Optimizing a Trainium2 kernel written in Bass

```python
import math
import jax, jax.numpy as jnp
from jax import lax
import numpy as np

D_MODEL = 4096
BATCH = 2
SEQ = 4096
DEPTH = 2

N_A_LAYERS = DEPTH // 2
N_B_LAYERS = DEPTH - N_A_LAYERS

A_GROUPS = ((128, 1), (512, 4), (2048, 16))
N_A_GROUPS = 3
A_HEADS_PER_GROUP = 16
A_HEAD_DIM = 128
A_QKV_WIDTH = 3 * N_A_GROUPS * A_HEADS_PER_GROUP * A_HEAD_DIM
A_OUT_WIDTH = A_HEADS_PER_GROUP * A_HEAD_DIM

N_BUCKETS = 32
REL_MAX_EXACT = N_BUCKETS // 2
REL_MAX_DISTANCE = 2048

B_HEADS = 32
Q_RANK = 1024
KV_RANK = 512
NOPE_DIM = 128
ROPE_DIM = 64
V_DIM = 128
QK_DIM = NOPE_DIM + ROPE_DIM
ROPE_THETA = 10000.0
MLA_BLOCK = 128

D_FF = 11008
CONV_WIDTH = 3

EPS = 1e-6

kernel_name = "yoco_dilated_mla_convffn_trunk"


def rmsnorm(x, gain):
    xf = x.astype(jnp.float32)
    y = xf * lax.rsqrt(jnp.mean(xf * xf, axis=-1, keepdims=True) + EPS)
    return (y * gain.astype(jnp.float32)).astype(x.dtype)


def t5_bucket(dist):
    small = dist < REL_MAX_EXACT
    d_f = jnp.maximum(dist, 1).astype(jnp.float32)
    large = REL_MAX_EXACT + (jnp.log(d_f / REL_MAX_EXACT) / math.log(REL_MAX_DISTANCE / REL_MAX_EXACT)
                             * (N_BUCKETS - REL_MAX_EXACT)).astype(jnp.int32)
    large = jnp.minimum(large, N_BUCKETS - 1)
    return jnp.where(small, dist, large)


def dilated_group(q, k, v, bias_tab, window, dilation):
    B_, S_, H, Dh = q.shape
    C = window // dilation
    n = S_ // dilation
    nc = -(-n // C)
    npad = nc * C

    def split(t):
        t = t.reshape(B_, n, dilation, H, Dh)
        t = jnp.pad(t, ((0, 0), (0, npad - n), (0, 0), (0, 0), (0, 0)))
        return t.reshape(B_, nc, C, dilation, H, Dh)

    def with_prev(t):
        prev = jnp.pad(t, ((0, 0), (1, 0), (0, 0), (0, 0), (0, 0), (0, 0)))[:, :-1]
        return jnp.concatenate([prev, t], axis=2)

    qc = split(q)
    kw = with_prev(split(k))
    vw = with_prev(split(v))

    logits = jnp.einsum('bcqrhd,bckrhd->bcrhqk', qc, kw).astype(jnp.float32) * (Dh ** -0.5)

    qi = jnp.arange(C, dtype=jnp.int32)[:, None]
    kk = jnp.arange(2 * C, dtype=jnp.int32)[None, :]
    steps = qi + C - kk
    band = (steps >= 0) & (steps <= C)
    bucket = t5_bucket(jnp.maximum(steps, 0) * dilation)
    bias = jnp.transpose(bias_tab[bucket].astype(jnp.float32), (2, 0, 1))
    blk = jnp.arange(nc, dtype=jnp.int32)[:, None]
    key_valid = (blk * C - C + kk) >= 0
    mask = band[None] & key_valid[:, None, :]

    logits = jnp.where(mask[None, :, None, None], logits + bias, -jnp.inf)
    m = jnp.max(logits, axis=-1, keepdims=True)
    p = jnp.exp(logits - m)
    l = jnp.sum(p, axis=-1, keepdims=True)
    lse = (m + jnp.log(l))[..., 0]
    o = jnp.einsum('bcrhqk,bckrhd->bcqrhd', (p / l).astype(v.dtype), vw)
    o = o.reshape(B_, npad, dilation, H, Dh)[:, :n].reshape(B_, S_, H, Dh)
    lse = jnp.transpose(lse, (0, 1, 4, 2, 3)).reshape(B_, npad, dilation, H)[:, :n].reshape(B_, S_, H)
    return o, lse


def dilated_mixer(h, norm_g, w_qkv, q_gain, k_gain, w_o, rel_bias):
    B_, S_, _ = h.shape
    hn = rmsnorm(h, norm_g)
    qkv = (hn @ w_qkv).reshape(B_, S_, 3, N_A_GROUPS, A_HEADS_PER_GROUP, A_HEAD_DIM)
    outs = []
    lses = []
    for g, (window, dilation) in enumerate(A_GROUPS):
        q = rmsnorm(qkv[:, :, 0, g], q_gain[g])
        k = rmsnorm(qkv[:, :, 1, g], k_gain[g])
        v = qkv[:, :, 2, g]
        o, lse = dilated_group(q, k, v, rel_bias[:, g], window, dilation)
        outs.append(o)
        lses.append(lse)
    mix = jax.nn.softmax(jnp.stack(lses, axis=0), axis=0)
    o = jnp.einsum('gbsh,gbshd->bshd', mix.astype(h.dtype), jnp.stack(outs, axis=0))
    return (o.reshape(B_, S_, A_OUT_WIDTH) @ w_o).astype(h.dtype)


def rope_tables(S_):
    pos = jnp.arange(S_, dtype=jnp.float32)
    inv = ROPE_THETA ** (-jnp.arange(0, ROPE_DIM, 2, dtype=jnp.float32) / ROPE_DIM)
    ang = pos[:, None] * inv[None, :]
    return jnp.cos(ang), jnp.sin(ang)


def apply_rope(t, cos, sin):
    tf = t.astype(jnp.float32)
    half = ROPE_DIM // 2
    t1, t2 = tf[..., :half], tf[..., half:]
    c = cos[None, :, None, :]
    s = sin[None, :, None, :]
    return jnp.concatenate([t1 * c - t2 * s, t2 * c + t1 * s], axis=-1).astype(t.dtype)


def qk_norm_rope(t, gain, cos, sin):
    t = rmsnorm(t, gain)
    return jnp.concatenate([t[..., :NOPE_DIM], apply_rope(t[..., NOPE_DIM:], cos, sin)], axis=-1)


def shared_latent_kv(h, kv_norm, kv_w_down, kv_latent_norm, kv_w_up, kv_k_norm, cos, sin):
    B_, S_, _ = h.shape
    hn = rmsnorm(h, kv_norm)
    ckr = hn @ kv_w_down
    c_kv = rmsnorm(ckr[..., :KV_RANK], kv_latent_norm)
    k_rope = ckr[..., KV_RANK:]
    kv = (c_kv @ kv_w_up).reshape(B_, S_, B_HEADS, NOPE_DIM + V_DIM)
    k_nope, v = kv[..., :NOPE_DIM], kv[..., NOPE_DIM:]
    k = jnp.concatenate([k_nope, jnp.broadcast_to(k_rope[:, :, None, :], (B_, S_, B_HEADS, ROPE_DIM))], axis=-1)
    k = qk_norm_rope(k, kv_k_norm, cos, sin)
    return k, v


def mla_mixer(h, norm_g, w_dq, q_latent_norm, w_uq, q_gain, w_o, k, v, cos, sin):
    B_, S_, _ = h.shape
    hn = rmsnorm(h, norm_g)
    cq = rmsnorm(hn @ w_dq, q_latent_norm)
    q = (cq @ w_uq).reshape(B_, S_, B_HEADS, QK_DIM)
    q = qk_norm_rope(q, q_gain, cos, sin)
    nb = S_ // MLA_BLOCK
    qb = jnp.transpose(q.reshape(B_, nb, MLA_BLOCK, B_HEADS, QK_DIM), (1, 0, 2, 3, 4))
    starts = jnp.arange(nb, dtype=jnp.int32) * MLA_BLOCK
    key_pos = jnp.arange(S_, dtype=jnp.int32)
    scale = QK_DIM ** -0.5

    def one_block(args):
        qblk, s0 = args
        logits = jnp.einsum('bqhd,bkhd->bhqk', qblk, k).astype(jnp.float32) * scale
        qpos = s0 + jnp.arange(MLA_BLOCK, dtype=jnp.int32)
        causal = key_pos[None, :] <= qpos[:, None]
        p = jax.nn.softmax(jnp.where(causal, logits, -jnp.inf), axis=-1)
        return jnp.einsum('bhqk,bkhd->bqhd', p.astype(v.dtype), v)

    out = lax.map(one_block, (qb, starts))
    out = jnp.transpose(out, (1, 0, 2, 3, 4)).reshape(B_, S_, B_HEADS * V_DIM)
    return (out @ w_o).astype(h.dtype)


def conv_ffn(h, norm_g, w_up, conv_w, conv_b, w_down):
    hn = rmsnorm(h, norm_g)
    u = hn @ w_up
    u = lax.conv_general_dilated(
        u, conv_w[:, None, :].astype(u.dtype), window_strides=(1,),
        padding=[(CONV_WIDTH - 1, 0)], dimension_numbers=('NWC', 'WIO', 'NWC'),
        feature_group_count=u.shape[-1]) + conv_b.astype(u.dtype)
    gate, up = jnp.split(u, 2, axis=-1)
    return ((jax.nn.silu(gate) * up) @ w_down).astype(h.dtype)


def setup_inputs(seed: int = 0) -> dict:
    key = jax.random.key(seed)
    ks = jax.random.split(key, 26)
    f32 = jnp.float32

    def w(k, shape, fan_in):
        return jax.random.normal(k, shape, f32) * (fan_in ** -0.5)

    def gain(k, shape):
        return 1.0 + 0.02 * jax.random.normal(k, shape, f32)

    return {
        "x": jax.random.normal(ks[0], (BATCH, SEQ, D_MODEL), f32),
        "a_attn_norm": gain(ks[1], (N_A_LAYERS, D_MODEL)),
        "a_w_qkv": w(ks[2], (N_A_LAYERS, D_MODEL, A_QKV_WIDTH), D_MODEL),
        "a_q_norm": gain(ks[3], (N_A_LAYERS, N_A_GROUPS, A_HEAD_DIM)),
        "a_k_norm": gain(ks[4], (N_A_LAYERS, N_A_GROUPS, A_HEAD_DIM)),
        "a_w_o": w(ks[5], (N_A_LAYERS, A_OUT_WIDTH, D_MODEL), A_OUT_WIDTH),
        "rel_bias": 0.5 * jax.random.normal(ks[6], (N_BUCKETS, N_A_GROUPS, A_HEADS_PER_GROUP), f32),
        "kv_norm": gain(ks[7], (D_MODEL,)),
        "kv_w_down": w(ks[8], (D_MODEL, KV_RANK + ROPE_DIM), D_MODEL),
        "kv_latent_norm": gain(ks[9], (KV_RANK,)),
        "kv_w_up": w(ks[10], (KV_RANK, B_HEADS * (NOPE_DIM + V_DIM)), KV_RANK),
        "kv_k_norm": gain(ks[11], (QK_DIM,)),
        "b_attn_norm": gain(ks[12], (N_B_LAYERS, D_MODEL)),
        "b_w_dq": w(ks[13], (N_B_LAYERS, D_MODEL, Q_RANK), D_MODEL),
        "b_q_latent_norm": gain(ks[14], (N_B_LAYERS, Q_RANK)),
        "b_w_uq": w(ks[15], (N_B_LAYERS, Q_RANK, B_HEADS * QK_DIM), Q_RANK),
        "b_q_norm": gain(ks[16], (N_B_LAYERS, QK_DIM)),
        "b_w_o": w(ks[17], (N_B_LAYERS, B_HEADS * V_DIM, D_MODEL), B_HEADS * V_DIM),
        "ffn_norm": gain(ks[18], (DEPTH, D_MODEL)),
        "ffn_w_up": w(ks[19], (DEPTH, D_MODEL, 2 * D_FF), D_MODEL),
        "ffn_conv_w": w(ks[20], (DEPTH, CONV_WIDTH, 2 * D_FF), CONV_WIDTH),
        "ffn_conv_b": 0.01 * jax.random.normal(ks[21], (DEPTH, 2 * D_FF), f32),
        "ffn_w_down": w(ks[22], (DEPTH, D_FF, D_MODEL), D_FF),
    }


def reference(x, a_attn_norm, a_w_qkv, a_q_norm, a_k_norm, a_w_o, rel_bias,
              kv_norm, kv_w_down, kv_latent_norm, kv_w_up, kv_k_norm,
              b_attn_norm, b_w_dq, b_q_latent_norm, b_w_uq, b_q_norm, b_w_o,
              ffn_norm, ffn_w_up, ffn_conv_w, ffn_conv_b, ffn_w_down):
    S_ = x.shape[1]
    cos, sin = rope_tables(S_)
    h = x
    k_shared = None
    v_shared = None
    for layer in range(DEPTH):
        if layer < N_A_LAYERS:
            h = h + dilated_mixer(h, a_attn_norm[layer], a_w_qkv[layer], a_q_norm[layer],
                                  a_k_norm[layer], a_w_o[layer], rel_bias)
        else:
            if layer == N_A_LAYERS:
                k_shared, v_shared = shared_latent_kv(h, kv_norm, kv_w_down, kv_latent_norm,
                                                      kv_w_up, kv_k_norm, cos, sin)
            i = layer - N_A_LAYERS
            h = h + mla_mixer(h, b_attn_norm[i], b_w_dq[i], b_q_latent_norm[i], b_w_uq[i],
                              b_q_norm[i], b_w_o[i], k_shared, v_shared, cos, sin)
        h = h + conv_ffn(h, ffn_norm[layer], ffn_w_up[layer], ffn_conv_w[layer],
                         ffn_conv_b[layer], ffn_w_down[layer])
    return h
```

```python
import math
import numpy as np
from contextlib import ExitStack
import concourse.bass as bass
import concourse.mybir as mybir
from concourse.bass_utils import run_bass_kernel_spmd

F32 = mybir.dt.float32
BF16 = mybir.dt.bfloat16
AF = mybir.ActivationFunctionType
ALU = mybir.AluOpType
AX = mybir.AxisListType
NEG = -30000.0
EPS = 1e-6


class StopBuild(Exception):
    pass


STOP = [99]
SEGMODE = [0]


def checkpoint(n):
    if STOP[0] <= n:
        raise StopBuild()


class Cfg:
    def __init__(self, D=4096, DFF=11008, AH=16, BH=32, QR=1024, KVR=512, SEQ=4096, BATCH=2, NF=4):
        self.D, self.DFF, self.AH, self.BH, self.QR, self.KVR = D, DFF, AH, BH, QR, KVR
        self.SEQ, self.BATCH, self.NF = SEQ, BATCH, NF
        self.T = 1024
        self.RPB = SEQ // self.T
        self.NCORE = BATCH * self.RPB
        self.KC = D // 128
        self.NFB = DFF // 128
        self.GROUPS = ((128, 1), (512, 4), (2048, 16))
        self.HALO = [128, 512, 2048]


FULL = Cfg()


class SemObj:
    def __init__(self, h, name):
        self.h, self.name, self.count = h, name, 0


class Buf:
    def __init__(self, t, sem=None):
        self.t = t
        self.w = None
        self.r = []
        self.sem = sem

    def __getitem__(self, idx):
        return self.t[idx]


class K:
    ENG = ("pe", "act", "dve", "pool", "sp")

    def __init__(self, nc, es):
        self.nc = nc
        self.e = {"pe": nc.tensor, "act": nc.scalar, "dve": nc.vector, "pool": nc.gpsimd, "sp": nc.sync}
        self.esem = {n: SemObj(es.enter_context(nc.semaphore("e_" + n)), n) for n in self.ENG}
        self.seen = {n: {} for n in self.ENG}
        self.dsems = []
        self.free_dsems = []
        self.es = es
        self.nbuf = 0

    def get_dsem(self):
        if self.free_dsems:
            return self.free_dsems.pop()
        s = SemObj(self.es.enter_context(self.nc.semaphore("d%d" % len(self.dsems))), "d%d" % len(self.dsems))
        self.dsems.append(s)
        return s

    def put_dsem(self, s):
        self.free_dsems.append(s)

    def wait(self, eng, ev):
        if ev is None:
            return
        so, val = ev
        if self.seen[eng].get(so.name, 0) >= val:
            return
        self.e[eng].wait_ge(so.h, val)
        self.seen[eng][so.name] = val

    def deps(self, eng, reads, writes):
        for b in reads:
            self.wait(eng, b.w)
        for b in writes:
            self.wait(eng, b.w)
            for ev in b.r:
                self.wait(eng, ev)

    def done(self, ev, reads, writes):
        for b in reads:
            b.r.append(ev)
        for b in writes:
            b.w = ev
            b.r = []

    def mark(self, eng, ins):
        so = self.esem[eng]
        ins.then_inc(so.h, 1)
        so.count += 1
        return (so, so.count)

    def op(self, eng, fn, reads=(), writes=()):
        self.deps(eng, reads, writes)
        ins = fn(self.e[eng])
        ev = self.mark(eng, ins)
        self.done(ev, reads, writes)
        return ev

    def dma(self, q, out, in_, reads=(), writes=(), sem=None):
        self.deps(q, reads, writes)
        if sem is None:
            for b in list(writes) + list(reads):
                if b.sem is not None:
                    sem = b.sem
                    break
        assert sem is not None
        self.e[q].dma_start(out=out, in_=in_).then_inc(sem.h, 16)
        sem.count += 16
        ev = (sem, sem.count)
        self.done(ev, reads, writes)
        return ev

    def barrier(self, bufs=()):
        for eng in self.ENG:
            for s in list(self.esem.values()) + self.dsems:
                if s.count > 0 and not (s.name == eng):
                    self.wait(eng, (s, s.count))
        for b in bufs:
            b.w = None
            b.r = []


class Phase:
    def __init__(self, k):
        self.k = k
        self.es = ExitStack()
        self.sems = []
        self.bufs = []

    def __enter__(self):
        self.es.__enter__()
        return self

    def __exit__(self, *a):
        self.k.barrier()
        for s in self.sems:
            self.k.put_dsem(s)
        return self.es.__exit__(*a)

    def sb(self, shape, dt, dma=False, name=None):
        self.k.nbuf += 1
        t = self.es.enter_context(self.k.nc.sbuf_tensor(name or ("sb%d" % self.k.nbuf), list(shape), dt))
        sem = None
        if dma:
            sem = self.k.get_dsem()
            self.sems.append(sem)
        b = Buf(t, sem)
        self.bufs.append(b)
        return b

    def ps(self, shape, dt=F32, name=None):
        self.k.nbuf += 1
        t = self.es.enter_context(self.k.nc.psum_tensor(name or ("ps%d" % self.k.nbuf), list(shape), dt))
        b = Buf(t)
        self.bufs.append(b)
        return b

    def dsem(self):
        s = self.k.get_dsem()
        self.sems.append(s)
        return s


def norm_transpose(k, C, src_rows, ntiles, gain_bc, dstT, col0, ident, pre_tile=None):
    D, KC = C.D, C.KC
    with Phase(k) as ph:
        xt = [ph.sb([128, D], F32, dma=True) for _ in range(2)]
        xb = [ph.sb([128, D], BF16) for _ in range(2)]
        junk = ph.sb([128, D], BF16)
        ss = [ph.sb([128, 1], F32) for _ in range(2)]
        rs = [ph.sb([128, 1], F32) for _ in range(2)]
        epsc = EPSC[0]
        tp = [ph.ps([128, 8, 128], BF16) for _ in range(2)]
        ng = 0
        for i in range(ntiles):
            if pre_tile is not None and i == 0:
                xi = pre_tile
            else:
                xi = xt[i % 2]
                k.dma("sp", xi[:], src_rows(i), writes=[xi])
            s, r, xbi = ss[i % 2], rs[i % 2], xb[i % 2]
            k.op("pool", lambda e: e.memset(s[:], 0.0), writes=[s])
            k.op("act", lambda e: e.activation(out=junk[:], in_=xi[:], func=AF.Square, scale=float(D) ** -0.5,
                                               accum_out=s[:]),
                 reads=[xi], writes=[junk, s])
            k.op("act", lambda e: e.activation(out=r[:], in_=s[:], func=AF.Sqrt, bias=epsc[:, 0:1]), reads=[s, epsc], writes=[r])
            k.op("dve", lambda e: e.reciprocal(out=r[:], in_=r[:]), reads=[r], writes=[r])
            k.op("dve", lambda e: e.scalar_tensor_tensor(out=xbi[:], in0=xi[:], scalar=r[:, 0:1], in1=gain_bc[:],
                                                         op0=ALU.mult, op1=ALU.mult),
                 reads=[xi, r, gain_bc], writes=[xbi])
            for g0 in range(0, KC, 8):
                n = min(8, KC - g0)
                tpj = tp[ng % 2]
                k.deps("pe", [xbi, ident], [tpj])
                for c in range(n):
                    ins = k.e["pe"].transpose(out=tpj[:, c, :], in_=xbi[:, (g0 + c) * 128:(g0 + c + 1) * 128],
                                              identity=ident[:])
                ev = k.mark("pe", ins)
                k.done(ev, [xbi, ident], [tpj])
                eng = "act" if ng % 2 == 0 else "dve"
                dst = dstT[:, g0:g0 + n, col0 + 128 * i: col0 + 128 * (i + 1)]
                if eng == "act":
                    k.op("act", lambda e: e.copy(out=dst, in_=tpj[:, 0:n, :]), reads=[tpj], writes=[dstT])
                else:
                    k.op("dve", lambda e: e.tensor_copy(out=dst, in_=tpj[:, 0:n, :]), reads=[tpj], writes=[dstT])
                ng += 1


def load_w(k, wt, pieces):
    for (ap, off) in pieces:
        n = ap.shape[-1]
        k.dma("pool", wt[:, :, off:off + n], ap.rearrange("(c p) n -> p c n", p=128), writes=[wt])


def gemm_b(k, ph, actT, chunks, KCn, items, epilogue, wring, psring, wcols=256):
    nw = len(wring)
    for i in range(min(nw, len(items))):
        load_w(k, wring[i], items[i]["pieces"])
    nb = 0
    for i, it in enumerate(items):
        wt = wring[i % nw]
        for (off, M, tag) in it["blocks"]:
            ps = psring[nb % len(psring)]
            nb += 1
            k.deps("pe", [wt, actT], [ps])
            for kc in range(KCn):
                for j, (c0, n) in enumerate(chunks):
                    ins = k.e["pe"].matmul(ps[0:M, j * 512:j * 512 + n], lhsT=wt[:, kc, off:off + M],
                                           rhs=actT[:, kc, c0:c0 + n], start=(kc == 0), stop=(kc == KCn - 1))
            ev = k.mark("pe", ins)
            k.done(ev, [wt, actT], [ps])
            epilogue(tag, ps)
        if i + nw < len(items):
            load_w(k, wring[(i + nw) % nw], items[i + nw]["pieces"])


def gemm_a(k, ph, actT, tts, KCn, items, epilogue, wring, psring):
    nw = len(wring)
    for i in range(min(nw, len(items))):
        load_w(k, wring[i], items[i]["pieces"])
    nb = 0
    for i, it in enumerate(items):
        wt = wring[i % nw]
        ncols = it["ncols"]
        for ti, c0 in enumerate(tts):
            ps = psring[nb % len(psring)]
            nb += 1
            k.deps("pe", [wt, actT], [ps])
            for kc in range(KCn):
                ins = k.e["pe"].matmul(ps[:, 0:ncols], lhsT=actT[:, kc, c0:c0 + 128], rhs=wt[:, kc, 0:ncols],
                                       start=(kc == 0), stop=(kc == KCn - 1))
            ev = k.mark("pe", ins)
            k.done(ev, [wt, actT], [ps])
            epilogue(it["tag"], ti, ps)
        if i + nw < len(items):
            load_w(k, wring[(i + nw) % nw], items[i + nw]["pieces"])


EPSC = [None]


def rstd_from(k, ssp, rbuf, ncols):
    epsc = EPSC[0]
    k.op("act", lambda e: e.activation(out=rbuf[:, 0:ncols], in_=ssp[:, 0:ncols], func=AF.Sqrt, bias=epsc[:, 0:1]),
         reads=[ssp, epsc], writes=[rbuf])
    k.op("dve", lambda e: e.reciprocal(out=rbuf[:, 0:ncols], in_=rbuf[:, 0:ncols]), reads=[rbuf], writes=[rbuf])


def headnorm(k, ps_list, ncols, gcols, sumw, eps_buf, sq_bufs, ssp, rbuf, outs):
    for (ps, M), sq in zip(ps_list, sq_bufs):
        k.op("act", lambda e: e.activation(out=sq[0:M, 0:ncols], in_=ps[0:M, 0:ncols], func=AF.Square),
             reads=[ps], writes=[sq])
    k.deps("pe", list(sq_bufs[:len(ps_list)]) + [w for w in sumw], [ssp])
    for c0 in range(0, ncols, 512):
        n = min(512, ncols - c0)
        for j, ((ps, M), sq, ow) in enumerate(zip(ps_list, sq_bufs, sumw)):
            ins = k.e["pe"].matmul(ssp[:, c0:c0 + n], lhsT=ow[0:M, :], rhs=sq[0:M, c0:c0 + n],
                                   start=(j == 0), stop=(j == len(ps_list) - 1))
    ev = k.mark("pe", ins)
    k.done(ev, list(sq_bufs[:len(ps_list)]) + [w for w in sumw], [ssp])
    rstd_from(k, ssp, rbuf, ncols)


def build(C, seg=0):
    SEGMODE[0] = seg
    nc = bass.Bass("TRN2", target_bir_lowering=False)
    D, T, KC, AH, BH, DFF, NFB, QR, KVR, RPB = C.D, C.T, C.KC, C.AH, C.BH, C.DFF, C.NFB, C.QR, C.KVR, C.RPB
    AW = AH * 128
    NQKV = 9 * AW

    def din(name, shape, dt=F32):
        return nc.dram_tensor(name, list(shape), dt, kind="ExternalInput").ap()

    def dout(name, shape, dt=F32):
        return nc.dram_tensor(name, list(shape), dt, kind="ExternalOutput").ap()

    def dscr(name, shape, dt):
        return nc.dram_tensor(name, list(shape), dt).ap()

    need = lambda *segs: seg == 0 or seg in segs
    NGC = 6 + QR // 128 + KVR // 128 + 2
    gbc = din("gbc", [5, 128, D])
    gcol = din("gcol", [128, NGC])
    grope = din("grope", [64, 4])
    ident_in = din("ident", [128, 128])
    hv_in = din("hv", [128, 3])
    if need(1):
        x = din("x", [T, D])
        xh = din("xh", [2 * T, D])
        w_qkv = din("a_w_qkv", [D, NQKV])
        w_oa = din("a_w_o", [AW, D])
        t5tab = din("t5tab", [3 * AH, 128, 256])
        QT = dscr("QT", [3 * AH, 128, T], BF16)
        KT = [dscr("KT%d" % g, [AH, 128, C.HALO[g] + T], BF16) for g in range(3)]
        VV = [dscr("VV%d" % g, [C.HALO[g] + T, AW], BF16) for g in range(3)]
        OA = [dscr("OA%d" % g, [T, AH, 129], F32) for g in range(3)]
    if need(2, 4):
        w_up = {L: din("ffn_w_up%d" % L, [D, 2 * DFF]) for L in (0, 1) if need(2 + 2 * L)}
        w_dn = {L: din("ffn_w_down%d" % L, [DFF, D]) for L in (0, 1) if need(2 + 2 * L)}
        convw = din("convw", [128, 2, 3, 2 * NFB])
        convb = din("convb", [128, 2, 2 * NFB])
        sel_in = din("sel", [2, C.NCORE])
        hal_in = dscr("hal_in", [2, D], F32)
        hal_out = dscr("hal_out", [2 * C.NCORE, D], F32) if seg == 0 else din("hal_g", [2 * C.NCORE, D])
    if need(2, 3):
        cos_in = din("cos2", [64, T])
        sin_in = din("sin2", [64, T])
    if need(2):
        kv_wd = din("kv_w_down", [D, KVR + 64])
        kv_wu = din("kv_w_up", [KVR, BH * 256])
        mk = dscr if seg == 0 else (lambda n, sh, dt: dout(n, sh, dt))
        KBn_l = mk("KBn_l", [BH * 128, T], BF16)
        KBr_l = mk("KBr_l", [BH * 64, T], BF16)
        VB_l = mk("VB_l", [T, BH * 128], BF16)
    if need(3):
        w_dq = din("b_w_dq", [D, QR])
        w_uq = din("b_w_uq", [QR, BH * 192])
        w_ob = din("b_w_o", [BH * 128, D])
        rbL_in = din("rbL", [128, RPB])
        rbG_in = din("rbG", [128, RPB])
        tmix_in = din("tmix", [128, RPB * 4 * 512])
        QBn = dscr("QBn", [BH, 128, T], BF16)
        QBr = dscr("QBr", [BH, 64, T], BF16)
        mk = dscr if seg == 0 else (lambda n, sh, dt: din(n, sh, dt))
        KBn_g = mk("KBn_g", [RPB * BH * 128, T], BF16)
        KBr_g = mk("KBr_g", [RPB * BH * 64, T], BF16)
        VB_g = mk("VB_g", [RPB * T, BH * 128], BF16)
    out = dout("out", [T, D])
    hin = din("hin", [T, D]) if seg >= 2 else None

    gs = ExitStack()
    with gs:
        k = K(nc, gs)
        cc_sem = gs.enter_context(nc.semaphore("cc_sem"))
        cc_count = [0]

        def allgather(src, dst, groups):
            if seg != 0:
                return
            nc.gpsimd.collective_compute("AllGather", ALU.bypass, replica_groups=groups,
                                         ins=[src.opt()], outs=[dst.opt()]).then_inc(cc_sem)
            cc_count[0] += 1
            nc.gpsimd.wait_ge(cc_sem, cc_count[0])
            k.op("pool", lambda e: e.memset(ccjunk[:], 0.0), writes=[ccjunk])
            k.barrier()

        try:
          with Phase(k) as gph:
            ident = gph.sb([128, 128], BF16)
            idf = gph.sb([128, 128], F32, dma=True)
            gc = gph.sb([128, NGC], F32, dma=True)
            gr = gph.sb([64, 4], F32, dma=True)
            hv = gph.sb([128, 3], F32, dma=True)
            ones_dh = gph.sb([128, 128], BF16)
            ones_192 = gph.sb([128, 128], BF16)
            ones_qr = gph.sb([128, 128], BF16)
            ones_kv = gph.sb([128, 128], BF16)
            ones_1 = gph.sb([128, 128], BF16)
            ccjunk = gph.sb([128, 1], F32)
            epsc = gph.sb([128, 1], F32)
            EPSC[0] = epsc
            k.op("pool", lambda e: e.memset(epsc[:], EPS), writes=[epsc])
            k.dma("sp", idf[:], ident_in, writes=[idf])
            k.dma("sp", gc[:], gcol, writes=[gc])
            k.dma("sp", gr[:], grope, writes=[gr])
            k.dma("sp", hv[:], hv_in, writes=[hv])
            k.op("dve", lambda e: e.tensor_copy(out=ident[:], in_=idf[:]), reads=[idf], writes=[ident])
            for (ob, val) in ((ones_dh, 1.0 / 128), (ones_192, 1.0 / 192), (ones_qr, 1.0 / QR),
                              (ones_kv, 1.0 / KVR), (ones_1, 1.0)):
                k.op("pool", lambda e: e.memset(ob[:], val), writes=[ob])
            if seg >= 2:
                csem = k.get_dsem()
                for i in range(0, T, 128):
                    k.dma("sp", out[i:i + 128, :], hin[i:i + 128, :], sem=csem)
            k.barrier()

            if need(1):
                layer_a(k, C, nc, x, xh, w_qkv, w_oa, gbc, gc, hv, t5tab, ident, ones_dh, ones_1, QT, KT, VV, OA, out)
            checkpoint(3)
            if need(2):
                ffn(k, C, nc, 0, out, w_up, w_dn, gbc[3], convw, convb, sel_in, hal_in, hal_out, ident, allgather)
            checkpoint(4)
            if need(2):
                layer_b_kv(k, C, nc, out, kv_wd, kv_wu, gbc, gc, gr, cos_in, sin_in, ident, ones_192, ones_kv,
                           KBn_l, KBr_l, VB_l)
            if seg == 0:
                k.barrier()
                groups = [list(range(b * RPB, (b + 1) * RPB)) for b in range(C.BATCH)]
                allgather(KBn_l, KBn_g, groups)
                allgather(KBr_l, KBr_g, groups)
                allgather(VB_l, VB_g, groups)
            if need(3):
                layer_b_q(k, C, nc, out, w_dq, w_uq, w_ob, gbc, gc, gr, cos_in, sin_in, rbL_in, rbG_in, tmix_in,
                          ident, ones_192, ones_qr, ones_1, QBn, QBr, KBn_g, KBr_g, VB_g)
            checkpoint(5)
            if need(4):
                ffn(k, C, nc, 1, out, w_up, w_dn, gbc[4], convw, convb, sel_in, hal_in, hal_out, ident, allgather)
            k.barrier()
        except StopBuild:
            k.barrier()
    return nc


def layer_a(k, C, nc, x, xh, w_qkv, w_oa, gbc, gc, hv, t5tab, ident, ones_dh, ones_1, QT, KT, VV, OA, out):
    D, T, KC, AH = C.D, C.T, C.KC, C.AH
    AW = AH * 128
    scale = 128 ** -0.5
    for pss in range(3):
        with Phase(k) as ph:
            hnT = ph.sb([128, KC, T], BF16)
            gain = ph.sb([128, D], F32, dma=True)
            k.dma("sp", gain[:], gbc[0], writes=[gain])
            if pss == 0:
                src = lambda i: x[i * 128:(i + 1) * 128, :]
            elif pss == 1:
                src = lambda i: xh[T + i * 128:T + (i + 1) * 128, :]
            else:
                src = lambda i: xh[i * 128:(i + 1) * 128, :]
            norm_transpose(k, C, src, T // 128, gain, hnT, 0, ident)
            checkpoint(0.5)
            if pss == 0:
                rng = {0: (0, T), 1: (0, T), 2: (0, T)}
            elif pss == 1:
                rng = {0: (T - 128, 128), 1: (T - 512, 512), 2: (0, T)}
            else:
                rng = {2: (0, T)}
            with Phase(k) as p2:
                wring = [p2.sb([128, KC, 256], BF16, dma=True) for _ in range(3)]
                psring = [p2.ps([128, 1024]) for _ in range(2)]
                ssp = p2.ps([128, 1024])
                sq = p2.sb([128, 1024], BF16)
                rb = p2.sb([128, 1024], F32)
                stg = [p2.sb([128, 1024], BF16, dma=True) for _ in range(2)]
                nst = [0]

                def qk_epi_factory(c0, n):
                    def epi(tag, ps):
                        kind, g, h = tag
                        headnorm(k, [(ps, 128)], n, None, [ones_dh], None, [sq], ssp, rb, None)
                        st = stg[nst[0] % 2]
                        nst[0] += 1
                        gcoln = (0 if kind == "q" else 3) + g
                        k.op("dve", lambda e: e.scalar_tensor_tensor(out=st[:, 0:n], in0=ps[:, 0:n],
                                                                     scalar=gc[:, gcoln:gcoln + 1], in1=rb[:, 0:n],
                                                                     op0=ALU.mult, op1=ALU.mult),
                             reads=[ps, rb, gc], writes=[st])
                        if kind == "q":
                            dst = QT[g * AH + h, :, 0:n]
                        else:
                            base = C.HALO[g] + {0: 0, 1: -T, 2: -2 * T}[pss] + c0
                            dst = KT[g][h, :, base:base + n]
                        k.dma("sp", dst, st[:, 0:n], reads=[st])
                    return epi

                by_rng = {}
                for g, (c0, n) in rng.items():
                    by_rng.setdefault((c0, n), []).append(g)
                for (c0, n), gl in by_rng.items():
                    blocks = []
                    for g in gl:
                        kinds = ("q", "k") if pss == 0 else ("k",)
                        for kind in kinds:
                            base = (0 if kind == "q" else 3 * AW) + g * AW
                            for h in range(AH):
                                blocks.append((base + h * 128, (kind, g, h)))
                    items = []
                    for i in range(0, len(blocks), 2):
                        pc, bl = [], []
                        for j, (col, tag) in enumerate(blocks[i:i + 2]):
                            pc.append((w_qkv[:, col:col + 128], j * 128))
                            bl.append((j * 128, 128, tag))
                        if len(pc) == 2 and blocks[i + 1][0] == blocks[i][0] + 128:
                            pc = [(w_qkv[:, blocks[i][0]:blocks[i][0] + 256], 0)]
                        items.append(dict(pieces=pc, blocks=bl))
                    chunks = [(c0 + o, min(512, n - o)) for o in range(0, n, 512)]
                    gemm_b(k, p2, hnT, chunks, KC, items, qk_epi_factory(c0, n), wring, psring)
            checkpoint(0.6)
            with Phase(k) as p3:
                wring = [p3.sb([128, KC, 512], BF16, dma=True) for _ in range(2)]
                psring = [p3.ps([128, 512]) for _ in range(4)]
                stg = [p3.sb([128, 512], BF16, dma=True) for _ in range(3)]
                nst = [0]
                for g, (c0, n) in rng.items():
                    tts = [c0 + o for o in range(0, n, 128)]
                    items = []
                    for pc0 in range(0, AW, 512):
                        ncols = min(512, AW - pc0)
                        col = 6 * AW + g * AW + pc0
                        pieces = [(w_qkv[:, col + o:col + min(o + 256, ncols)], o) for o in range(0, ncols, 256)]
                        items.append(dict(pieces=pieces, ncols=ncols, tag=(g, pc0, ncols)))

                    def v_epi(tag, ti, ps, c0=c0):
                        g_, pc0, ncols = tag
                        st = stg[nst[0] % 3]
                        nst[0] += 1
                        if nst[0] % 2 == 0:
                            k.op("act", lambda e: e.copy(out=st[:, 0:ncols], in_=ps[:, 0:ncols]), reads=[ps], writes=[st])
                        else:
                            k.op("dve", lambda e: e.tensor_copy(out=st[:, 0:ncols], in_=ps[:, 0:ncols]), reads=[ps], writes=[st])
                        row = C.HALO[g_] + {0: 0, 1: -T, 2: -2 * T}[pss] + c0 + ti * 128
                        k.dma("sp", VV[g_][row:row + 128, pc0:pc0 + ncols], st[:, 0:ncols], reads=[st])

                    gemm_a(k, p3, hnT, tts, KC, items, v_epi, wring, psring)
            checkpoint(0.7)

    checkpoint(1)
    for g, (window, d) in enumerate(C.GROUPS):
        H = C.HALO[g]
        nq = T // d
        QB = min(128, nq)
        nblk = nq // QB
        with Phase(k) as ph:
            nkc = (128 + nq + 127) // 128
            vt = ph.sb([128, d * nkc, AW], BF16, dma=True)
            for r in range(d):
                for c in range(nkc):
                    nk = min(128, nq + 128 - 128 * c)
                    row0 = H + r + d * (-128 + 128 * c)
                    srcv = VV[g][row0:row0 + d * (nk - 1) + 1:d, :] if d > 1 else VV[g][row0:row0 + nk, :]
                    k.dma("sp", vt[0:nk, r * nkc + c, :], srcv, writes=[vt])
            kt = [ph.sb([128, H + T], BF16, dma=True) for _ in range(2)]
            qt = [ph.sb([128, T], BF16, dma=True) for _ in range(2)]
            tab = [ph.sb([128, 256], F32, dma=True) for _ in range(2)]
            tab0 = [ph.sb([128, 256], F32) for _ in range(2)]
            sps = [ph.ps([128, 256]) for _ in range(2)]
            ops = [ph.ps([128, 132]) for _ in range(2)]
            tmp = [ph.sb([128, 256], F32) for _ in range(2)]
            pt = [ph.sb([128, 256], BF16) for _ in range(2)]
            ost = [ph.sb([128, 132], F32, dma=True) for _ in range(3)]
            ntile = 0
            for h in range(AH):
                kth, qth, tb, tb0 = kt[h % 2], qt[h % 2], tab[h % 2], tab0[h % 2]
                k.dma("sp", kth[:], KT[g][h], writes=[kth])
                k.dma("sp", qth[:], QT[g * AH + h], writes=[qth])
                k.dma("sp", tb[:], t5tab[g * AH + h], writes=[tb])
                k.op("pool", lambda e: e.tensor_copy(out=tb0[:, 128:256], in_=tb[:, 128:256]), reads=[tb], writes=[tb0])
                k.op("dve", lambda e: e.tensor_scalar(out=tb0[:, 0:128], in0=tb[:, 0:128], scalar1=hv[:, g:g + 1],
                                                      scalar2=None, op0=ALU.add), reads=[tb, hv, tb0], writes=[tb0])
                for r in range(d):
                    for qb in range(nblk):
                        i0 = qb * QB
                        sp_, op_, tm, p_ = sps[ntile % 2], ops[ntile % 2], tmp[ntile % 2], pt[ntile % 2]
                        os_ = ost[ntile % 3]
                        ntile += 1
                        tbu = tb0 if qb == 0 else tb
                        ka0 = H + r + d * (i0 - 128)
                        kb0 = H + r + d * i0
                        q0 = r + d * i0

                        def sl(t_, s0, n):
                            return t_[:, s0:s0 + d * (n - 1) + 1:d] if d > 1 else t_[:, s0:s0 + n]
                        k.deps("pe", [kth, qth], [sp_])
                        k.e["pe"].matmul(sp_[:, 0:QB], lhsT=sl(kth, ka0, 128), rhs=sl(qth, q0, QB), start=True, stop=True)
                        ins = k.e["pe"].matmul(sp_[0:QB, 128:128 + QB], lhsT=sl(kth, kb0, QB), rhs=sl(qth, q0, QB),
                                               start=True, stop=True)
                        ev = k.mark("pe", ins)
                        k.done(ev, [kth, qth], [sp_])
                        if QB == 128:
                            k.op("dve", lambda e: e.scalar_tensor_tensor(out=tm[:, :], in0=sp_[:, :], scalar=scale,
                                                                         in1=tbu[:, :], op0=ALU.mult, op1=ALU.add),
                                 reads=[sp_, tbu], writes=[tm])
                            k.op("act", lambda e: e.activation(out=p_[:, :], in_=tm[:, :], func=AF.Exp),
                                 reads=[tm], writes=[p_])
                        else:
                            k.op("dve", lambda e: e.scalar_tensor_tensor(out=tm[:, 0:QB], in0=sp_[:, 0:QB], scalar=scale,
                                                                         in1=tbu[:, 0:QB], op0=ALU.mult, op1=ALU.add),
                                 reads=[sp_, tbu], writes=[tm])
                            k.op("dve", lambda e: e.scalar_tensor_tensor(out=tm[0:QB, 128:128 + QB], in0=sp_[0:QB, 128:128 + QB],
                                                                         scalar=scale, in1=tbu[0:QB, 128:128 + QB],
                                                                         op0=ALU.mult, op1=ALU.add),
                                 reads=[sp_, tbu, tm], writes=[tm])
                            k.op("act", lambda e: e.activation(out=p_[:, 0:QB], in_=tm[:, 0:QB], func=AF.Exp),
                                 reads=[tm], writes=[p_])
                            k.op("act", lambda e: e.activation(out=p_[0:QB, 128:128 + QB], in_=tm[0:QB, 128:128 + QB], func=AF.Exp),
                                 reads=[tm, p_], writes=[p_])
                        ca = (i0 // 128) if QB == 128 else 0
                        va = vt[:, r * nkc + ca, h * 128:(h + 1) * 128]
                        vb = vt[0:QB, r * nkc + ca + 1, h * 128:(h + 1) * 128]
                        k.deps("pe", [p_, vt, ones_1], [op_])
                        k.e["pe"].matmul(op_[0:QB, 0:128], lhsT=p_[:, 0:QB], rhs=va, start=True, stop=False)
                        k.e["pe"].matmul(op_[0:QB, 0:128], lhsT=p_[0:QB, 128:128 + QB], rhs=vb, start=False, stop=True)
                        k.e["pe"].matmul(op_[0:QB, 128:129], lhsT=p_[:, 0:QB], rhs=ones_1[:, 0:1], start=True, stop=False)
                        ins = k.e["pe"].matmul(op_[0:QB, 128:129], lhsT=p_[0:QB, 128:128 + QB], rhs=ones_1[0:QB, 0:1],
                                               start=False, stop=True)
                        ev = k.mark("pe", ins)
                        k.done(ev, [p_, vt, ones_1], [op_])
                        k.op("act", lambda e: e.copy(out=os_[0:QB, 0:129], in_=op_[0:QB, 0:129]), reads=[op_], writes=[os_])
                        t0 = r + d * i0
                        dsto = OA[g][t0:t0 + d * (QB - 1) + 1:d, h, :] if d > 1 else OA[g][t0:t0 + QB, h, :]
                        k.dma("sp", dsto, os_[0:QB, 0:129], reads=[os_])

    checkpoint(2)
    with Phase(k) as ph:
        oT = ph.sb([128, AH, T], BF16)
        with Phase(k) as p2:
            oa = [[p2.sb([128, AH, 129], F32, dma=True) for _ in range(3)] for _ in range(2)]
            rl = [p2.sb([128, AH, 1], F32) for _ in range(2)]
            ob = [p2.sb([128, AH, 128], BF16) for _ in range(2)]
            tp = [p2.ps([128, 8, 128], BF16) for _ in range(2)]
            ng = 0
            for tt in range(T // 128):
                o3 = oa[tt % 2]
                for g in range(3):
                    k.dma("sp", o3[g][:], OA[g][tt * 128:(tt + 1) * 128], writes=[o3[g]])
                k.op("dve", lambda e: e.tensor_tensor(out=o3[0][:], in0=o3[0][:], in1=o3[1][:], op=ALU.add),
                     reads=[o3[0], o3[1]], writes=[o3[0]])
                k.op("dve", lambda e: e.tensor_tensor(out=o3[0][:], in0=o3[0][:], in1=o3[2][:], op=ALU.add),
                     reads=[o3[0], o3[2]], writes=[o3[0]])
                rli, obi = rl[tt % 2], ob[tt % 2]
                k.op("dve", lambda e: e.reciprocal(out=rli[:], in_=o3[0][:, :, 128:129]), reads=[o3[0]], writes=[rli])
                for h in range(AH):
                    eng = "act" if h % 2 == 0 else "dve"
                    if eng == "act":
                        k.op("act", lambda e: e.activation(out=obi[:, h, :], in_=o3[0][:, h, 0:128], func=AF.Copy,
                                                           scale=rli[:, h, 0:1]), reads=[o3[0], rli], writes=[obi])
                    else:
                        k.op("dve", lambda e: e.tensor_scalar(out=obi[:, h, :], in0=o3[0][:, h, 0:128],
                                                              scalar1=rli[:, h, 0:1], scalar2=None, op0=ALU.mult),
                             reads=[o3[0], rli], writes=[obi])
                for g0 in range(0, AH, 8):
                    n = min(8, AH - g0)
                    tpj = tp[ng % 2]
                    ng += 1
                    k.deps("pe", [obi, ident], [tpj])
                    for c in range(n):
                        ins = k.e["pe"].transpose(out=tpj[:, c, :], in_=obi[:, g0 + c, :], identity=ident[:])
                    ev = k.mark("pe", ins)
                    k.done(ev, [obi, ident], [tpj])
                    k.op("act", lambda e: e.copy(out=oT[:, g0:g0 + n, tt * 128:(tt + 1) * 128], in_=tpj[:, 0:n, :]),
                         reads=[tpj], writes=[oT])
        checkpoint(2.5)
        proj_residual(k, C, oT, AH, w_oa, x, out)


def proj_residual(k, C, actT, KCn, w, res_src, out):
    D, T = C.D, C.T
    with Phase(k) as p3:
        wring = [p3.sb([128, KCn, 512], BF16, dma=True) for _ in range(2)]
        psring = [p3.ps([128, 512]) for _ in range(4)]
        rs = [p3.sb([128, 512], F32, dma=True) for _ in range(3)]
        nst = [0]
        items = []
        for pc0 in range(0, D, 512):
            pieces = [(w[:, pc0 + o:pc0 + o + 256], o) for o in range(0, 512, 256)]
            items.append(dict(pieces=pieces, ncols=512, tag=pc0))

        def epi(pc0, ti, ps):
            r_ = rs[nst[0] % 3]
            nst[0] += 1
            k.dma("sp", r_[:], res_src[ti * 128:(ti + 1) * 128, pc0:pc0 + 512], writes=[r_])
            k.op("dve", lambda e: e.tensor_tensor(out=r_[:], in0=ps[:, 0:512], in1=r_[:], op=ALU.add),
                 reads=[ps, r_], writes=[r_])
            k.dma("sp", out[ti * 128:(ti + 1) * 128, pc0:pc0 + 512], r_[:], reads=[r_])

        gemm_a(k, p3, actT, [i * 128 for i in range(T // 128)], KCn, items, epi, wring, psring)


def ffn(k, C, nc, L, out, w_up, w_dn, gain_src, convw, convb, sel_in, hal_in, hal_out, ident, allgather):
    D, T, KC, DFF, NFB, NF = C.D, C.T, C.KC, C.DFF, C.NFB, C.NF
    W = T + 2
    k.barrier()
    if hal_in is not None and SEGMODE[0] == 0:
        hsem = k.get_dsem()
        k.dma("pool", hal_in, out[T - 2:T, :], sem=hsem)
        k.barrier()
        allgather(hal_in, hal_out, [list(range(C.NCORE))])
        k.put_dsem(hsem)
    with Phase(k) as ph:
        hnT = ph.sb([128, KC, W], BF16)
        gain = ph.sb([128, D], F32, dma=True)
        k.dma("sp", gain[:], gain_src, writes=[gain])
        with Phase(k) as p1:
            NCR = C.NCORE
            HC = min(D, 512)
            sel = p1.sb([128, NCR], F32, dma=True)
            htile = p1.sb([128, D], F32)
            k.dma("sp", sel[0:2, :], sel_in, writes=[sel])
            k.op("pool", lambda e: e.memset(htile[:], 0.0), writes=[htile])
            halv = hal_out.rearrange("(r i) d -> i r d", i=2)
            with Phase(k) as p1a:
                hgs = [p1a.sb([128, NCR, HC], F32, dma=True) for _ in range(2)]
                for ci, c0 in enumerate(range(0, D, HC)):
                    hg = hgs[ci % 2]
                    k.dma("pool", hg[0:2, :, :], halv[:, :, c0:c0 + HC], writes=[hg])
                    for r_ in range(NCR):
                        k.op("dve", lambda e: e.scalar_tensor_tensor(out=htile[0:2, c0:c0 + HC], in0=hg[0:2, r_, :],
                                                                     scalar=sel[0:2, r_:r_ + 1], in1=htile[0:2, c0:c0 + HC],
                                                                     op0=ALU.mult, op1=ALU.add),
                             reads=[hg, sel, htile], writes=[htile])
            checkpoint(3.3)
            with Phase(k) as p1b:
                hT = p1b.sb([128, KC, 128], BF16)
                norm_transpose(k, C, None, 1, gain, hT, 0, ident, pre_tile=htile)
                k.op("dve", lambda e: e.tensor_copy(out=hnT[:, :, 0:2], in_=hT[:, :, 0:2]), reads=[hT], writes=[hnT])
        norm_transpose(k, C, lambda i: out[i * 128:(i + 1) * 128, :], T // 128, gain, hnT, 2, ident)

        checkpoint(3.4)
        cw = ph.sb([128, 3, 2 * NFB], F32, dma=True)
        cb = ph.sb([128, 2 * NFB], F32, dma=True)
        k.dma("sp", cw[:], convw[:, L], writes=[cw])
        k.dma("sp", cb[:], convb[:, L], writes=[cb])
        fsplit = [NFB * i // NF for i in range(NF + 1)]
        for qf in range(NF):
            f0, f1 = fsplit[qf], fsplit[qf + 1]
            nfq = f1 - f0
            with Phase(k) as pq:
                GT = pq.sb([128, nfq, T], BF16)
                with Phase(k) as p2:
                    wring = [p2.sb([128, KC, 256], BF16, dma=True) for _ in range(2)]
                    psring = [p2.ps([128, 1536]) for _ in range(2)]
                    ga = [p2.sb([128, T], F32) for _ in range(2)]
                    ua = [p2.sb([128, T], F32) for _ in range(2)]
                    gsl = [p2.sb([128, T], F32) for _ in range(2)]
                    items = []
                    for f in range(f0, f1):
                        items.append(dict(pieces=[(w_up[L][:, f * 128:(f + 1) * 128], 0),
                                                  (w_up[L][:, DFF + f * 128:DFF + (f + 1) * 128], 128)],
                                          blocks=[(0, 128, ("g", f)), (128, 128, ("u", f))]))

                    def up_epi(tag, ps):
                        kind, f = tag
                        col = f if kind == "g" else NFB + f
                        a = (ga if kind == "g" else ua)[f % 2]
                        k.op("act", lambda e: e.activation(out=a[:], in_=ps[:, 2:T + 2], func=AF.Identity,
                                                           bias=cb[:, col:col + 1], scale=cw[:, 2, col:col + 1]),
                             reads=[ps, cb, cw], writes=[a])
                        k.op("dve", lambda e: e.scalar_tensor_tensor(out=a[:], in0=ps[:, 1:T + 1], scalar=cw[:, 1, col:col + 1],
                                                                     in1=a[:], op0=ALU.mult, op1=ALU.add),
                             reads=[ps, cw, a], writes=[a])
                        k.op("dve", lambda e: e.scalar_tensor_tensor(out=a[:], in0=ps[:, 0:T], scalar=cw[:, 0, col:col + 1],
                                                                     in1=a[:], op0=ALU.mult, op1=ALU.add),
                             reads=[ps, cw, a], writes=[a])
                        if kind == "g":
                            s_ = gsl[f % 2]
                            k.op("act", lambda e: e.activation(out=s_[:], in_=a[:], func=AF.Silu), reads=[a], writes=[s_])
                        else:
                            s_ = gsl[f % 2]
                            k.op("pool", lambda e: e.tensor_tensor(out=GT[:, f - f0, :], in0=s_[:], in1=a[:], op=ALU.mult),
                                 reads=[s_, a], writes=[GT])

                    chunks = [(0, 512), (512, 512), (1024, 2)]
                    gemm_b(k, p2, hnT, chunks, KC, items, up_epi, wring, psring)
                checkpoint(3.5)
                with Phase(k) as p3:
                    wring = [p3.sb([128, nfq, 512], BF16, dma=True) for _ in range(2)]
                    psring = [p3.ps([128, 512]) for _ in range(4)]
                    rs = [p3.sb([128, 512], F32, dma=True) for _ in range(3)]
                    nst = [0]
                    items = []
                    for pc0 in range(0, D, 512):
                        pieces = [(w_dn[L][f0 * 128:f1 * 128, pc0 + o:pc0 + o + 256], o) for o in range(0, 512, 256)]
                        items.append(dict(pieces=pieces, ncols=512, tag=pc0))

                    def dn_epi(pc0, ti, ps):
                        r_ = rs[nst[0] % 3]
                        nst[0] += 1
                        k.dma("sp", r_[:], out[ti * 128:(ti + 1) * 128, pc0:pc0 + 512], writes=[r_])
                        k.op("dve", lambda e: e.tensor_tensor(out=r_[:], in0=ps[:, 0:512], in1=r_[:], op=ALU.add),
                             reads=[ps, r_], writes=[r_])
                        k.dma("sp", out[ti * 128:(ti + 1) * 128, pc0:pc0 + 512], r_[:], reads=[r_])

                    gemm_a(k, p3, GT, [i * 128 for i in range(T // 128)], nfq, items, dn_epi, wring, psring)


def layer_b_kv(k, C, nc, out, kv_wd, kv_wu, gbc, gc, gr, cos_in, sin_in, ident, ones_192, ones_kv, KBn_l, KBr_l, VB_l):
    D, T, KC, BH, QR, KVR, RPB = C.D, C.T, C.KC, C.BH, C.QR, C.KVR, C.RPB
    QC, KVC = QR // 128, KVR // 128
    scale = 192 ** -0.5
    GQL, GKL, GQN, GKN = 6, 6 + QC, 6 + QC + KVC, 6 + QC + KVC + 1
    with Phase(k) as ph:
        tC = [None, None]
        tS = [None, None]
        i = 1
        tC[i] = ph.sb([64, T], F32, dma=True)
        tS[i] = ph.sb([64, T], F32, dma=True)
        k.dma("sp", tC[i][:], cos_in, writes=[tC[i]])
        k.dma("sp", tS[i][:], sin_in, writes=[tS[i]])
        k.op("dve", lambda e: e.tensor_scalar(out=tC[i][:], in0=tC[i][:], scalar1=gr[:, 2 * i:2 * i + 1], scalar2=None,
                                              op0=ALU.mult), reads=[tC[i], gr], writes=[tC[i]])
        k.op("dve", lambda e: e.tensor_scalar(out=tS[i][:], in0=tS[i][:], scalar1=gr[:, 2 * i + 1:2 * i + 2], scalar2=None,
                                              op0=ALU.mult), reads=[tS[i], gr], writes=[tS[i]])
        sq = [ph.sb([128, T], BF16) for _ in range(1)]
        rb = ph.sb([128, T], F32)
        ckvT = ph.sb([128, KVC, T], BF16)
        krope = ph.sb([64, T], F32)
        sqkr = ph.sb([64, T], BF16)
        with Phase(k) as p1:
            hnT = p1.sb([128, KC, T], BF16)
            gain = p1.sb([128, D], F32, dma=True)
            k.dma("sp", gain[:], gbc[1], writes=[gain])
            norm_transpose(k, C, lambda i: out[i * 128:(i + 1) * 128, :], T // 128, gain, hnT, 0, ident)
            with Phase(k) as p2:
                wring = [p2.sb([128, KC, 256], BF16, dma=True) for _ in range(3)]
                psring = [p2.ps([128, 1024]) for _ in range(2)]
                ssp = p2.ps([128, 1024])
                raw = p2.sb([128, KVC, T], F32)
                kr_raw = p2.sb([64, T], F32)
                kr_sw = p2.sb([64, T], F32)
                items = []
                for c in range(0, KVC, 2):
                    n = min(2, KVC - c)
                    items.append(dict(pieces=[(kv_wd[:, c * 128:(c + n) * 128], 0)],
                                      blocks=[(j * 128, 128, ("c", c + j)) for j in range(n)]))
                items.append(dict(pieces=[(kv_wd[:, KVR:KVR + 64], 0), (kv_wd[:, KVR + 32:KVR + 64], 64),
                                          (kv_wd[:, KVR:KVR + 32], 96)],
                                  blocks=[(0, 64, ("r", 0)), (64, 64, ("s", 0))]))

                def kvd_epi(tag, ps):
                    kind, c = tag
                    if kind == "c":
                        k.op("act", lambda e: e.copy(out=raw[:, c, :], in_=ps[:, 0:T]), reads=[ps], writes=[raw])
                    elif kind == "r":
                        k.op("act", lambda e: e.copy(out=kr_raw[:], in_=ps[0:64, 0:T]), reads=[ps], writes=[kr_raw])
                    else:
                        k.op("act", lambda e: e.copy(out=kr_sw[:], in_=ps[0:64, 0:T]), reads=[ps], writes=[kr_sw])

                gemm_b(k, p2, hnT, [(0, 512), (512, 512)], KC, items, kvd_epi, wring, psring)
                sqc = [p2.sb([128, T], BF16) for _ in range(2)]
                for c in range(KVC):
                    sqb = sqc[c % 2]
                    k.op("act", lambda e: e.activation(out=sqb[:], in_=raw[:, c, :], func=AF.Square),
                         reads=[raw], writes=[sqb])
                    k.deps("pe", [sqb, ones_kv], [ssp] if c == 0 else [])
                    for c0 in range(0, T, 512):
                        ins = k.e["pe"].matmul(ssp[:, c0:c0 + 512], lhsT=ones_kv[:, :], rhs=sqb[:, c0:c0 + 512],
                                               start=(c == 0), stop=(c == KVC - 1))
                    ev = k.mark("pe", ins)
                    k.done(ev, [sqb, ones_kv], [])
                ssp.w = ev
                ssp.r = []
                rstd_from(k, ssp, rb, T)
                for c in range(KVC):
                    k.op("dve", lambda e: e.scalar_tensor_tensor(out=ckvT[:, c, :], in0=raw[:, c, :],
                                                                 scalar=gc[:, GKL + c:GKL + c + 1], in1=rb[:],
                                                                 op0=ALU.mult, op1=ALU.mult),
                         reads=[raw, gc, rb], writes=[ckvT])
                k.op("act", lambda e: e.activation(out=sqkr[:], in_=kr_raw[:], func=AF.Square), reads=[kr_raw], writes=[sqkr])
                k.op("dve", lambda e: e.tensor_tensor(out=krope[:], in0=kr_raw[:], in1=tC[1][:], op=ALU.mult),
                     reads=[kr_raw, tC[1]], writes=[krope])
                k.op("dve", lambda e: e.tensor_tensor(out=kr_sw[:], in0=kr_sw[:], in1=tS[1][:], op=ALU.mult),
                     reads=[kr_sw, tS[1]], writes=[kr_sw])
                k.op("dve", lambda e: e.tensor_tensor(out=krope[:], in0=krope[:], in1=kr_sw[:], op=ALU.add),
                     reads=[krope, kr_sw], writes=[krope])

        with Phase(k) as p2:
            wring = [p2.sb([128, KVC, 256], BF16, dma=True) for _ in range(3)]
            psring = [p2.ps([128, 1024]) for _ in range(2)]
            ssp = p2.ps([128, 1024])
            stn = [p2.sb([128, T], BF16, dma=True) for _ in range(2)]
            strp = [p2.sb([64, T], BF16, dma=True) for _ in range(2)]
            items = []
            for h in range(0, BH, 2):
                items.append(dict(pieces=[(kv_wu[:, h * 256:h * 256 + 128], 0), (kv_wu[:, (h + 1) * 256:(h + 1) * 256 + 128], 128)],
                                  blocks=[(0, 128, h), (128, 128, h + 1)]))

            def k_epi(h, ps):
                k.op("act", lambda e: e.activation(out=sq[0][:], in_=ps[:, 0:T], func=AF.Square), reads=[ps], writes=[sq[0]])
                k.deps("pe", [sq[0], sqkr, ones_192], [ssp])
                for c0 in range(0, T, 512):
                    k.e["pe"].matmul(ssp[:, c0:c0 + 512], lhsT=ones_192[:, :], rhs=sq[0][:, c0:c0 + 512], start=True, stop=False)
                    ins = k.e["pe"].matmul(ssp[:, c0:c0 + 512], lhsT=ones_192[0:64, :], rhs=sqkr[:, c0:c0 + 512],
                                           start=False, stop=True)
                ev = k.mark("pe", ins)
                k.done(ev, [sq[0], sqkr, ones_192], [ssp])
                rstd_from(k, ssp, rb, T)
                sn, sr = stn[h % 2], strp[h % 2]
                k.op("dve", lambda e: e.scalar_tensor_tensor(out=sn[:], in0=ps[:, 0:T], scalar=gc[:, GKN:GKN + 1], in1=rb[:],
                                                             op0=ALU.mult, op1=ALU.mult), reads=[ps, gc, rb], writes=[sn])
                k.op("pool", lambda e: e.tensor_tensor(out=sr[:], in0=krope[:], in1=rb[0:64, :], op=ALU.mult),
                     reads=[krope, rb], writes=[sr])
                k.dma("sp", KBn_l[h * 128:(h + 1) * 128, :], sn[:], reads=[sn])
                k.dma("sp", KBr_l[h * 64:(h + 1) * 64, :], sr[:], reads=[sr])

            gemm_b(k, p2, ckvT, [(0, 512), (512, 512)], KVC, items, k_epi, wring, psring)
        with Phase(k) as p3:
            wring = [p3.sb([128, KVC, 512], BF16, dma=True) for _ in range(2)]
            psring = [p3.ps([128, 512]) for _ in range(4)]
            stg = [p3.sb([128, 512], BF16, dma=True) for _ in range(3)]
            nst = [0]
            items = []
            for h0 in range(0, BH, 4):
                nh = min(4, BH - h0)
                items.append(dict(pieces=[(kv_wu[:, (h0 + j) * 256 + 128:(h0 + j) * 256 + 256], j * 128) for j in range(nh)],
                                  ncols=nh * 128, tag=(h0, nh)))

            def vb_epi(tag, ti, ps):
                h0, nh = tag
                st = stg[nst[0] % 3]
                nst[0] += 1
                k.op("act", lambda e: e.copy(out=st[:, 0:nh * 128], in_=ps[:, 0:nh * 128]), reads=[ps], writes=[st])
                k.dma("sp", VB_l[ti * 128:(ti + 1) * 128, h0 * 128:(h0 + nh) * 128], st[:, 0:nh * 128], reads=[st])

            gemm_a(k, p3, ckvT, [i * 128 for i in range(T // 128)], KVC, items, vb_epi, wring, psring)


def layer_b_q(k, C, nc, out, w_dq, w_uq, w_ob, gbc, gc, gr, cos_in, sin_in, rbL_in, rbG_in, tmix_in,
              ident, ones_192, ones_qr, ones_1, QBn, QBr, KBn_g, KBr_g, VB_g):
    D, T, KC, BH, QR, KVR, RPB = C.D, C.T, C.KC, C.BH, C.QR, C.KVR, C.RPB
    QC, KVC = QR // 128, KVR // 128
    scale = 192 ** -0.5
    GQL, GKL, GQN, GKN = 6, 6 + QC, 6 + QC + KVC, 6 + QC + KVC + 1
    with Phase(k) as ph:
        tC = [None, None]
        tS = [None, None]
        i = 0
        tC[i] = ph.sb([64, T], F32, dma=True)
        tS[i] = ph.sb([64, T], F32, dma=True)
        k.dma("sp", tC[i][:], cos_in, writes=[tC[i]])
        k.dma("sp", tS[i][:], sin_in, writes=[tS[i]])
        k.op("dve", lambda e: e.tensor_scalar(out=tC[i][:], in0=tC[i][:], scalar1=gr[:, 2 * i:2 * i + 1], scalar2=None,
                                              op0=ALU.mult), reads=[tC[i], gr], writes=[tC[i]])
        k.op("dve", lambda e: e.tensor_scalar(out=tS[i][:], in0=tS[i][:], scalar1=gr[:, 2 * i + 1:2 * i + 2], scalar2=None,
                                              op0=ALU.mult), reads=[tS[i], gr], writes=[tS[i]])
        sq = [ph.sb([128, T], BF16) for _ in range(2)]
        rb = ph.sb([128, T], F32)
        cqT = ph.sb([128, QC, T], BF16)
        with Phase(k) as p1:
            hnT = p1.sb([128, KC, T], BF16)
            gain = p1.sb([128, D], F32, dma=True)
            k.dma("sp", gain[:], gbc[2], writes=[gain])
            norm_transpose(k, C, lambda i: out[i * 128:(i + 1) * 128, :], T // 128, gain, hnT, 0, ident)
            with Phase(k) as p2:
                wring = [p2.sb([128, KC, 256], BF16, dma=True) for _ in range(2)]
                psring = [p2.ps([128, 1024]) for _ in range(2)]
                ssp = p2.ps([128, 1024])
                raw = p2.sb([128, QC, T], F32)
                items = []
                for c in range(0, QC, 2):
                    n = min(2, QC - c)
                    items.append(dict(pieces=[(w_dq[:, c * 128:(c + n) * 128], 0)],
                                      blocks=[(j * 128, 128, c + j) for j in range(n)]))

                def dq_epi(c, ps):
                    k.op("act", lambda e: e.copy(out=raw[:, c, :], in_=ps[:, 0:T]), reads=[ps], writes=[raw])

                gemm_b(k, p2, hnT, [(0, 512), (512, 512)], KC, items, dq_epi, wring, psring)
                sqc = [p2.sb([128, T], BF16) for _ in range(2)]
                for c in range(QC):
                    sqb = sqc[c % 2]
                    k.op("act", lambda e: e.activation(out=sqb[:], in_=raw[:, c, :], func=AF.Square),
                         reads=[raw], writes=[sqb])
                    k.deps("pe", [sqb, ones_qr], [ssp] if c == 0 else [])
                    for c0 in range(0, T, 512):
                        ins = k.e["pe"].matmul(ssp[:, c0:c0 + 512], lhsT=ones_qr[:, :], rhs=sqb[:, c0:c0 + 512],
                                               start=(c == 0), stop=(c == QC - 1))
                    ev = k.mark("pe", ins)
                    k.done(ev, [sqb, ones_qr], [])
                ssp.w = ev
                ssp.r = []
                rstd_from(k, ssp, rb, T)
                for c in range(QC):
                    k.op("dve", lambda e: e.scalar_tensor_tensor(out=cqT[:, c, :], in0=raw[:, c, :],
                                                                 scalar=gc[:, GQL + c:GQL + c + 1], in1=rb[:],
                                                                 op0=ALU.mult, op1=ALU.mult),
                         reads=[raw, gc, rb], writes=[cqT])
        with Phase(k) as p2:
            wring = [p2.sb([128, QC, 256], BF16, dma=True) for _ in range(3)]
            psring = [p2.ps([128, 1024]) for _ in range(3)]
            ssp = p2.ps([128, 1024])
            stn = [p2.sb([128, T], BF16, dma=True) for _ in range(2)]
            strp = [p2.sb([64, T], BF16, dma=True) for _ in range(2)]
            ra = p2.sb([64, T], F32)
            rbb = p2.sb([64, T], F32)
            items = []
            for h in range(BH):
                c = h * 192
                items.append(dict(pieces=[(w_uq[:, c:c + 192], 0), (w_uq[:, c + 160:c + 192], 192), (w_uq[:, c + 128:c + 160], 224)],
                                  blocks=[(0, 128, ("n", h)), (128, 64, ("r", h)), (192, 64, ("s", h))]))
            hold = {}

            def uq_epi(tag, ps):
                kind, h = tag
                hold[kind] = ps
                if kind != "s":
                    return
                pn, pr, psw = hold["n"], hold["r"], hold["s"]
                k.op("act", lambda e: e.activation(out=sq[0][:], in_=pn[:, 0:T], func=AF.Square), reads=[pn], writes=[sq[0]])
                k.op("act", lambda e: e.activation(out=sq[1][0:64, :], in_=pr[0:64, 0:T], func=AF.Square), reads=[pr], writes=[sq[1]])
                k.deps("pe", [sq[0], sq[1], ones_192], [ssp])
                for c0 in range(0, T, 512):
                    k.e["pe"].matmul(ssp[:, c0:c0 + 512], lhsT=ones_192[:, :], rhs=sq[0][:, c0:c0 + 512], start=True, stop=False)
                    ins = k.e["pe"].matmul(ssp[:, c0:c0 + 512], lhsT=ones_192[0:64, :], rhs=sq[1][0:64, c0:c0 + 512],
                                           start=False, stop=True)
                ev = k.mark("pe", ins)
                k.done(ev, [sq[0], sq[1], ones_192], [ssp])
                rstd_from(k, ssp, rb, T)
                sn, sr = stn[h % 2], strp[h % 2]
                k.op("dve", lambda e: e.scalar_tensor_tensor(out=sn[:], in0=pn[:, 0:T], scalar=gc[:, GQN:GQN + 1], in1=rb[:],
                                                             op0=ALU.mult, op1=ALU.mult), reads=[pn, gc, rb], writes=[sn])
                k.op("dve", lambda e: e.tensor_tensor(out=ra[:], in0=pr[0:64, 0:T], in1=tC[0][:], op=ALU.mult),
                     reads=[pr, tC[0]], writes=[ra])
                k.op("dve", lambda e: e.tensor_tensor(out=rbb[:], in0=psw[0:64, 0:T], in1=tS[0][:], op=ALU.mult),
                     reads=[psw, tS[0]], writes=[rbb])
                k.op("pool", lambda e: e.tensor_tensor(out=ra[:], in0=ra[:], in1=rbb[:], op=ALU.add), reads=[ra, rbb], writes=[ra])
                k.op("pool", lambda e: e.tensor_tensor(out=sr[:], in0=ra[:], in1=rb[0:64, :], op=ALU.mult),
                     reads=[ra, rb], writes=[sr])
                k.dma("sp", QBn[h], sn[:], reads=[sn])
                k.dma("sp", QBr[h], sr[:], reads=[sr])

            gemm_b(k, p2, cqT, [(0, 512), (512, 512)], QC, items, uq_epi, wring, psring)

    with Phase(k) as ph:
        aT = ph.sb([128, BH, T], BF16)
        with Phase(k) as p2:
            rbL = p2.sb([128, RPB], F32, dma=True)
            rbG = p2.sb([128, RPB], F32, dma=True)
            tmix = p2.sb([128, RPB * 4 * 512], F32, dma=True)
            k.dma("sp", rbL[:], rbL_in, writes=[rbL])
            k.dma("sp", rbG[:], rbG_in, writes=[rbG])
            k.dma("sp", tmix[:], tmix_in, writes=[tmix])
            NKT = RPB * (T // 128)
            ktn = [p2.sb([128, RPB * T], BF16, dma=True) for _ in range(2)]
            ktr = [p2.sb([64, RPB * T], BF16, dma=True) for _ in range(2)]
            vth = [p2.sb([128, NKT, 128], BF16, dma=True) for _ in range(2)]
            qn = [p2.sb([128, T], BF16, dma=True) for _ in range(2)]
            qr = [p2.sb([64, T], BF16, dma=True) for _ in range(2)]
            sps = [p2.ps([128, 512]) for _ in range(3)]
            ops = [p2.ps([128, 512]) for _ in range(2)]
            lps = [p2.ps([128, 512]) for _ in range(2)]
            tm = [p2.sb([128, 512], F32) for _ in range(2)]
            pt = [p2.sb([128, 512], BF16) for _ in range(3)]
            rl = [p2.sb([128, 512], F32) for _ in range(2)]
            nS = 0
            nO = 0
            for h in range(BH):
                a, b_, v_, qn_, qr_ = ktn[h % 2], ktr[h % 2], vth[h % 2], qn[h % 2], qr[h % 2]
                for kr in range(RPB):
                    k.dma("sp", a[:, kr * T:(kr + 1) * T], KBn_g[(kr * BH + h) * 128:(kr * BH + h + 1) * 128, :], writes=[a])
                    k.dma("sp", b_[:, kr * T:(kr + 1) * T], KBr_g[(kr * BH + h) * 64:(kr * BH + h + 1) * 64, :], writes=[b_])
                k.dma("sp", v_[:], VB_g[:, h * 128:(h + 1) * 128].rearrange("(c p) n -> p c n", p=128), writes=[v_])
                k.dma("sp", qn_[:], QBn[h], writes=[qn_])
                k.dma("sp", qr_[:], QBr[h], writes=[qr_])
                for qc in range(T // 512):
                    op_, lp_ = ops[nO % 2], lps[nO % 2]
                    rl_ = rl[nO % 2]
                    nO += 1
                    k.deps("pe", [], [op_, lp_])
                    for kt in range(NKT):
                        kr, ktl = kt // (T // 128), kt % (T // 128)
                        sp_ = sps[nS % 3]
                        p_ = pt[nS % 3]
                        tm_ = tm[nS % 2]
                        nS += 1
                        k.deps("pe", [a, b_, qn_, qr_], [sp_])
                        k.e["pe"].matmul(sp_[:, :], lhsT=a[:, kt * 128:(kt + 1) * 128], rhs=qn_[:, qc * 512:(qc + 1) * 512],
                                         start=True, stop=False)
                        ins = k.e["pe"].matmul(sp_[:, :], lhsT=b_[:, kt * 128:(kt + 1) * 128], rhs=qr_[:, qc * 512:(qc + 1) * 512],
                                               start=False, stop=True)
                        ev = k.mark("pe", ins)
                        k.done(ev, [a, b_, qn_, qr_], [sp_])
                        rel = ktl - 4 * qc
                        if rel < 0:
                            k.op("act", lambda e: e.activation(out=p_[:], in_=sp_[:], func=AF.Exp, bias=rbL[:, kr:kr + 1], scale=scale),
                                 reads=[sp_, rbL], writes=[p_])
                        elif rel > 3:
                            k.op("act", lambda e: e.activation(out=p_[:], in_=sp_[:], func=AF.Exp, bias=rbG[:, kr:kr + 1], scale=scale),
                                 reads=[sp_, rbG], writes=[p_])
                        else:
                            o_ = (kr * 4 + rel) * 512
                            k.op("dve", lambda e: e.scalar_tensor_tensor(out=tm_[:], in0=sp_[:], scalar=scale, in1=tmix[:, o_:o_ + 512],
                                                                         op0=ALU.mult, op1=ALU.add), reads=[sp_, tmix], writes=[tm_])
                            k.op("act", lambda e: e.activation(out=p_[:], in_=tm_[:], func=AF.Exp), reads=[tm_], writes=[p_])
                        k.deps("pe", [p_, v_, ones_1], [])
                        k.e["pe"].matmul(op_[:, :], lhsT=v_[:, kt, :], rhs=p_[:], start=(kt == 0), stop=(kt == NKT - 1))
                        ins = k.e["pe"].matmul(lp_[:, :], lhsT=ones_1[:, :], rhs=p_[:], start=(kt == 0), stop=(kt == NKT - 1))
                        ev = k.mark("pe", ins)
                        k.done(ev, [p_, v_, ones_1], [])
                    op_.w = ev
                    lp_.w = ev
                    op_.r = []
                    lp_.r = []
                    k.op("dve", lambda e: e.reciprocal(out=rl_[:], in_=lp_[:]), reads=[lp_], writes=[rl_])
                    k.op("dve", lambda e: e.tensor_tensor(out=aT[:, h, qc * 512:(qc + 1) * 512], in0=op_[:], in1=rl_[:], op=ALU.mult),
                         reads=[op_, rl_], writes=[aT])
        proj_residual(k, C, aT, BH, w_ob, out, out)


def t5_bucket_np(dist):
    dist = np.asarray(dist, dtype=np.int64)
    NB, ME, MD = 32, 16, 2048
    d_f = np.maximum(dist, 1).astype(np.float32)
    large = ME + (np.log(d_f / np.float32(ME)) / np.float32(math.log(MD / ME)) * np.float32(NB - ME)).astype(np.int32)
    large = np.minimum(large, NB - 1)
    return np.where(dist < ME, dist, large).astype(np.int64)


def prep_inputs(C, inp):
    D, T, AH, BH, NFB, RPB, QR, KVR = C.D, C.T, C.AH, C.BH, C.NFB, C.RPB, C.QR, C.KVR
    f32 = np.float32
    A = {kk: np.asarray(v) for kk, v in inp.items()}
    shared = {}
    shared["a_w_qkv"] = np.ascontiguousarray(A["a_w_qkv"][0])
    shared["a_w_o"] = np.ascontiguousarray(A["a_w_o"][0])
    shared["kv_w_down"] = np.ascontiguousarray(A["kv_w_down"])
    shared["kv_w_up"] = np.ascontiguousarray(A["kv_w_up"])
    shared["b_w_dq"] = np.ascontiguousarray(A["b_w_dq"][0])
    shared["b_w_uq"] = np.ascontiguousarray(A["b_w_uq"][0])
    shared["b_w_o"] = np.ascontiguousarray(A["b_w_o"][0])
    for L in (0, 1):
        shared["ffn_w_up%d" % L] = np.ascontiguousarray(A["ffn_w_up"][L])
        shared["ffn_w_down%d" % L] = np.ascontiguousarray(A["ffn_w_down"][L])
    rows = [A["a_attn_norm"][0], A["kv_norm"], A["b_attn_norm"][0], A["ffn_norm"][0], A["ffn_norm"][1]]
    shared["gbc"] = np.ascontiguousarray(np.stack([np.broadcast_to(r[None, :], (128, D)) for r in rows]).astype(f32))
    cols = [A["a_q_norm"][0, g] for g in range(3)] + [A["a_k_norm"][0, g] for g in range(3)]
    cols += [A["b_q_latent_norm"][0][c * 128:(c + 1) * 128] for c in range(QR // 128)]
    cols += [A["kv_latent_norm"][c * 128:(c + 1) * 128] for c in range(KVR // 128)]
    cols += [A["b_q_norm"][0][:128], A["kv_k_norm"][:128]]
    shared["gcol"] = np.ascontiguousarray(np.stack(cols, axis=1).astype(f32))
    qg, kg = A["b_q_norm"][0][128:192], A["kv_k_norm"][128:192]
    sw = lambda v: np.concatenate([v[32:], v[:32]])
    shared["grope"] = np.ascontiguousarray(np.stack([qg, sw(qg), kg, sw(kg)], axis=1).astype(f32))
    cw = A["ffn_conv_w"]
    shared["convw"] = np.ascontiguousarray(cw.reshape(2, 3, 2 * NFB, 128).transpose(3, 0, 1, 2).astype(f32))
    shared["convb"] = np.ascontiguousarray(A["ffn_conv_b"].reshape(2, 2 * NFB, 128).transpose(2, 0, 1).astype(f32))
    a = np.arange(128)[:, None]
    q = np.arange(128)[None, :]
    tabs = np.empty((3 * AH, 128, 256), f32)
    rel = A["rel_bias"]
    for g, (_, d) in enumerate(C.GROUPS):
        stA = q - a + 128
        stB = q - a
        bA = t5_bucket_np(np.maximum(stA, 0) * d)
        bB = t5_bucket_np(np.maximum(stB, 0) * d)
        for h in range(AH):
            tabs[g * AH + h, :, 0:128] = np.where(a >= q, rel[bA, g, h], f32(NEG))
            tabs[g * AH + h, :, 128:256] = np.where(a <= q, rel[bB, g, h], f32(NEG))
    shared["t5tab"] = tabs
    shared["ident"] = np.eye(128, dtype=f32)
    x = A["x"]
    in_maps = []
    for c in range(C.NCORE):
        b, jb = c // RPB, c % RPB
        s0 = jb * T
        m = dict(shared)
        m["x"] = np.ascontiguousarray(x[b, s0:s0 + T])
        xh = np.zeros((2 * T, D), f32)
        lo = s0 - 2 * T
        if lo >= 0:
            xh[:] = x[b, lo:s0]
        elif s0 > 0:
            xh[2 * T - s0:] = x[b, 0:s0]
        m["xh"] = xh
        hvv = np.full((128, 3), 0.0 if jb > 0 else NEG, f32)
        for g_, H_ in enumerate(C.HALO):
            d_ = C.GROUPS[g_][1]
            relpos = -(128 - np.arange(128)) * d_
            hvv[:, g_] = np.where(s0 + relpos >= 0, 0.0, NEG)
        m["hv"] = hvv
        sel = np.zeros((2, C.NCORE), f32)
        if jb > 0:
            sel[:, c - 1] = 1.0
        m["sel"] = sel
        pos = np.arange(s0, s0 + T, dtype=f32)
        inv = (f32(10000.0) ** (-np.arange(0, 64, 2, dtype=f32) / f32(64))).astype(f32)
        ang = (pos[None, :] * inv[:, None]).astype(f32)
        cs, sn = np.cos(ang).astype(f32), np.sin(ang).astype(f32)
        m["cos2"] = np.ascontiguousarray(np.concatenate([cs, cs], axis=0))
        m["sin2"] = np.ascontiguousarray(np.concatenate([-sn, sn], axis=0))
        rbL = np.zeros((128, RPB), f32)
        rbG = np.zeros((128, RPB), f32)
        tmix = np.zeros((128, RPB, 4, 512), f32)
        kk = np.arange(128)[:, None]
        qq = np.arange(512)[None, :]
        for kr in range(RPB):
            rbL[:, kr] = 0.0 if kr <= jb else NEG
            rbG[:, kr] = 0.0 if kr < jb else NEG
            for rel_ in range(4):
                if kr < jb:
                    tmix[:, kr, rel_] = 0.0
                elif kr > jb:
                    tmix[:, kr, rel_] = NEG
                else:
                    tmix[:, kr, rel_] = np.where(rel_ * 128 + kk <= qq, 0.0, NEG)
        m["rbL"], m["rbG"] = rbL, rbG
        m["tmix"] = np.ascontiguousarray(tmix.reshape(128, -1))
        in_maps.append(m)
    return in_maps


def input_names(nc):
    names = []
    for alloc in nc.allocations:
        if isinstance(alloc, mybir.MemoryLocationSet) and alloc.kind == "ExternalInput":
            names.append(alloc.memorylocations[0].name)
    return names


def launch(C, seg, in_maps, trace=False):
    nc = build(C, seg)
    names = set(input_names(nc))
    maps = [{kk: v for kk, v in m.items() if kk in names} for m in in_maps]
    missing = names - set(maps[0].keys())
    missing = {n for n in missing if "partition" not in n}
    assert not missing, missing
    res = run_bass_kernel_spmd(nc, maps, core_ids=list(range(C.NCORE)), trace=trace)
    return res


def run(C, inputs, trace=False, fused=False, upto=4):
    in_maps = prep_inputs(C, inputs)
    NCR, T, RPB = C.NCORE, C.T, C.RPB
    if fused:
        res = launch(C, 0, in_maps, trace)
        outs = [res.results[c]["out"] for c in range(NCR)]
        return np.stack(outs).reshape(C.BATCH, C.SEQ, C.D).astype(np.float32), res

    def halo_rows(hs):
        return np.ascontiguousarray(np.concatenate([h[T - 2:T] for h in hs], axis=0))

    res = launch(C, 1, in_maps, trace)
    hs = [res.results[c]["out"] for c in range(NCR)]
    if upto >= 2:
        hg = halo_rows(hs)
        for c in range(NCR):
            in_maps[c]["hin"] = hs[c]
            in_maps[c]["hal_g"] = hg
        res = launch(C, 2, in_maps, trace)
        hs = [res.results[c]["out"] for c in range(NCR)]
    if upto >= 3:
        for b in range(C.BATCH):
            rk = range(b * RPB, (b + 1) * RPB)
            kn = np.ascontiguousarray(np.concatenate([res.results[r]["KBn_l"] for r in rk], axis=0))
            kr_ = np.ascontiguousarray(np.concatenate([res.results[r]["KBr_l"] for r in rk], axis=0))
            vv = np.ascontiguousarray(np.concatenate([res.results[r]["VB_l"] for r in rk], axis=0))
            for c in rk:
                in_maps[c]["KBn_g"], in_maps[c]["KBr_g"], in_maps[c]["VB_g"] = kn, kr_, vv
        for c in range(NCR):
            in_maps[c]["hin"] = hs[c]
        res = launch(C, 3, in_maps, trace)
        hs = [res.results[c]["out"] for c in range(NCR)]
    if upto >= 4:
        hg = halo_rows(hs)
        for c in range(NCR):
            in_maps[c]["hin"] = hs[c]
            in_maps[c]["hal_g"] = hg
        res = launch(C, 4, in_maps, trace)
        hs = [res.results[c]["out"] for c in range(NCR)]
    y = np.stack(hs).reshape(C.BATCH, C.SEQ, C.D).astype(np.float32)
    return y, res


def kernel(**inputs):
    y, _ = run(FULL, inputs)
    return y
```

```python
import math
import numpy as np
from contextlib import ExitStack
import concourse.bass as bass
import concourse.mybir as mybir
from concourse.bass_utils import run_bass_kernel_spmd

F32 = mybir.dt.float32
BF16 = mybir.dt.bfloat16
AF = mybir.ActivationFunctionType
ALU = mybir.AluOpType
AX = mybir.AxisListType
NEG = -30000.0
EPS = 1e-6


class StopBuild(Exception):
    pass


STOP = [99]
SEGMODE = [0]


def checkpoint(n):
    if STOP[0] <= n:
        raise StopBuild()


class Cfg:
    def __init__(self, D=4096, DFF=11008, AH=16, BH=32, QR=1024, KVR=512, SEQ=4096, BATCH=2, NF=4):
        self.D, self.DFF, self.AH, self.BH, self.QR, self.KVR = D, DFF, AH, BH, QR, KVR
        self.SEQ, self.BATCH, self.NF = SEQ, BATCH, NF
        self.T = 1024
        self.RPB = SEQ // self.T
        self.NCORE = BATCH * self.RPB
        self.KC = D // 128
        self.NFB = DFF // 128
        self.GROUPS = ((128, 1), (512, 4), (2048, 16))
        self.HALO = [128, 512, 2048]


FULL = Cfg()


class SemObj:
    def __init__(self, h, name):
        self.h, self.name, self.count = h, name, 0


class Buf:
    def __init__(self, t, sem=None):
        self.t = t
        self.w = None
        self.r = []
        self.sem = sem

    def __getitem__(self, idx):
        return self.t[idx]


class K:
    ENG = ("pe", "act", "dve", "pool", "sp")

    def __init__(self, nc, es):
        self.nc = nc
        self.e = {"pe": nc.tensor, "act": nc.scalar, "dve": nc.vector, "pool": nc.gpsimd, "sp": nc.sync}
        self.esem = {n: SemObj(es.enter_context(nc.semaphore("e_" + n)), n) for n in self.ENG}
        self.seen = {n: {} for n in self.ENG}
        self.dsems = []
        self.free_dsems = []
        self.es = es
        self.nbuf = 0

    def get_dsem(self):
        if self.free_dsems:
            return self.free_dsems.pop()
        s = SemObj(self.es.enter_context(self.nc.semaphore("d%d" % len(self.dsems))), "d%d" % len(self.dsems))
        self.dsems.append(s)
        return s

    def put_dsem(self, s):
        self.free_dsems.append(s)

    def wait(self, eng, ev):
        if ev is None:
            return
        so, val = ev
        if self.seen[eng].get(so.name, 0) >= val:
            return
        self.e[eng].wait_ge(so.h, val)
        self.seen[eng][so.name] = val

    def deps(self, eng, reads, writes):
        for b in reads:
            self.wait(eng, b.w)
        for b in writes:
            self.wait(eng, b.w)
            for ev in b.r:
                self.wait(eng, ev)

    def done(self, ev, reads, writes):
        for b in reads:
            b.r.append(ev)
        for b in writes:
            b.w = ev
            b.r = []

    def mark(self, eng, ins):
        so = self.esem[eng]
        ins.then_inc(so.h, 1)
        so.count += 1
        return (so, so.count)

    def op(self, eng, fn, reads=(), writes=()):
        self.deps(eng, reads, writes)
        ins = fn(self.e[eng])
        ev = self.mark(eng, ins)
        self.done(ev, reads, writes)
        return ev

    def dma(self, q, out, in_, reads=(), writes=(), sem=None):
        self.deps(q, reads, writes)
        if sem is None:
            for b in list(writes) + list(reads):
                if b.sem is not None:
                    sem = b.sem
                    break
        assert sem is not None
        self.e[q].dma_start(out=out, in_=in_).then_inc(sem.h, 16)
        sem.count += 16
        ev = (sem, sem.count)
        self.done(ev, reads, writes)
        return ev

    def barrier(self, bufs=()):
        for eng in self.ENG:
            for s in list(self.esem.values()) + self.dsems:
                if s.count > 0 and not (s.name == eng):
                    self.wait(eng, (s, s.count))
        for b in bufs:
            b.w = None
            b.r = []


class Phase:
    def __init__(self, k):
        self.k = k
        self.es = ExitStack()
        self.sems = []
        self.bufs = []

    def __enter__(self):
        self.es.__enter__()
        return self

    def __exit__(self, *a):
        self.k.barrier()
        for s in self.sems:
            self.k.put_dsem(s)
        return self.es.__exit__(*a)

    def sb(self, shape, dt, dma=False, name=None):
        self.k.nbuf += 1
        t = self.es.enter_context(self.k.nc.sbuf_tensor(name or ("sb%d" % self.k.nbuf), list(shape), dt))
        sem = None
        if dma:
            sem = self.k.get_dsem()
            self.sems.append(sem)
        b = Buf(t, sem)
        self.bufs.append(b)
        return b

    def ps(self, shape, dt=F32, name=None):
        self.k.nbuf += 1
        t = self.es.enter_context(self.k.nc.psum_tensor(name or ("ps%d" % self.k.nbuf), list(shape), dt))
        b = Buf(t)
        self.bufs.append(b)
        return b

    def dsem(self):
        s = self.k.get_dsem()
        self.sems.append(s)
        return s


def norm_transpose(k, C, src_rows, ntiles, gain_bc, dstT, col0, ident, pre_tile=None):
    D, KC = C.D, C.KC
    with Phase(k) as ph:
        xt = [ph.sb([128, D], F32, dma=True) for _ in range(2)]
        xb = [ph.sb([128, D], BF16) for _ in range(2)]
        junk = ph.sb([128, D], BF16)
        ss = [ph.sb([128, 1], F32) for _ in range(2)]
        rs = [ph.sb([128, 1], F32) for _ in range(2)]
        epsc = EPSC[0]
        tp = [ph.ps([128, 8, 128], BF16) for _ in range(2)]
        ng = 0
        for i in range(ntiles):
            if pre_tile is not None and i == 0:
                xi = pre_tile
            else:
                xi = xt[i % 2]
                k.dma("sp", xi[:], src_rows(i), writes=[xi])
            s, r, xbi = ss[i % 2], rs[i % 2], xb[i % 2]
            k.op("pool", lambda e: e.memset(s[:], 0.0), writes=[s])
            k.op("act", lambda e: e.activation(out=junk[:], in_=xi[:], func=AF.Square, scale=float(D) ** -0.5,
                                               accum_out=s[:]),
                 reads=[xi], writes=[junk, s])
            k.op("act", lambda e: e.activation(out=r[:], in_=s[:], func=AF.Sqrt, bias=epsc[:, 0:1]), reads=[s, epsc], writes=[r])
            k.op("dve", lambda e: e.reciprocal(out=r[:], in_=r[:]), reads=[r], writes=[r])
            k.op("dve", lambda e: e.scalar_tensor_tensor(out=xbi[:], in0=xi[:], scalar=r[:, 0:1], in1=gain_bc[:],
                                                         op0=ALU.mult, op1=ALU.mult),
                 reads=[xi, r, gain_bc], writes=[xbi])
            for g0 in range(0, KC, 8):
                n = min(8, KC - g0)
                tpj = tp[ng % 2]
                k.deps("pe", [xbi, ident], [tpj])
                for c in range(n):
                    ins = k.e["pe"].transpose(out=tpj[:, c, :], in_=xbi[:, (g0 + c) * 128:(g0 + c + 1) * 128],
                                              identity=ident[:])
                ev = k.mark("pe", ins)
                k.done(ev, [xbi, ident], [tpj])
                eng = "act" if ng % 2 == 0 else "dve"
                dst = dstT[:, g0:g0 + n, col0 + 128 * i: col0 + 128 * (i + 1)]
                if eng == "act":
                    k.op("act", lambda e: e.copy(out=dst, in_=tpj[:, 0:n, :]), reads=[tpj], writes=[dstT])
                else:
                    k.op("dve", lambda e: e.tensor_copy(out=dst, in_=tpj[:, 0:n, :]), reads=[tpj], writes=[dstT])
                ng += 1


def load_w(k, wt, pieces):
    for (ap, off) in pieces:
        n = ap.shape[-1]
        k.dma("pool", wt[:, :, off:off + n], ap.rearrange("(c p) n -> p c n", p=128), writes=[wt])


def gemm_b(k, ph, actT, chunks, KCn, items, epilogue, wring, psring, wcols=256):
    nw = len(wring)
    for i in range(min(nw, len(items))):
        load_w(k, wring[i], items[i]["pieces"])
    nb = 0
    for i, it in enumerate(items):
        wt = wring[i % nw]
        for (off, M, tag) in it["blocks"]:
            ps = psring[nb % len(psring)]
            nb += 1
            k.deps("pe", [wt, actT], [ps])
            for kc in range(KCn):
                for j, (c0, n) in enumerate(chunks):
                    ins = k.e["pe"].matmul(ps[0:M, j * 512:j * 512 + n], lhsT=wt[:, kc, off:off + M],
                                           rhs=actT[:, kc, c0:c0 + n], start=(kc == 0), stop=(kc == KCn - 1))
            ev = k.mark("pe", ins)
            k.done(ev, [wt, actT], [ps])
            epilogue(tag, ps)
        if i + nw < len(items):
            load_w(k, wring[(i + nw) % nw], items[i + nw]["pieces"])


def gemm_a(k, ph, actT, tts, KCn, items, epilogue, wring, psring):
    nw = len(wring)
    for i in range(min(nw, len(items))):
        load_w(k, wring[i], items[i]["pieces"])
    nb = 0
    for i, it in enumerate(items):
        wt = wring[i % nw]
        ncols = it["ncols"]
        for ti, c0 in enumerate(tts):
            ps = psring[nb % len(psring)]
            nb += 1
            k.deps("pe", [wt, actT], [ps])
            for kc in range(KCn):
                ins = k.e["pe"].matmul(ps[:, 0:ncols], lhsT=actT[:, kc, c0:c0 + 128], rhs=wt[:, kc, 0:ncols],
                                       start=(kc == 0), stop=(kc == KCn - 1))
            ev = k.mark("pe", ins)
            k.done(ev, [wt, actT], [ps])
            epilogue(it["tag"], ti, ps)
        if i + nw < len(items):
            load_w(k, wring[(i + nw) % nw], items[i + nw]["pieces"])


EPSC = [None]


def rstd_from(k, ssp, rbuf, ncols):
    epsc = EPSC[0]
    k.op("act", lambda e: e.activation(out=rbuf[:, 0:ncols], in_=ssp[:, 0:ncols], func=AF.Sqrt, bias=epsc[:, 0:1]),
         reads=[ssp, epsc], writes=[rbuf])
    k.op("dve", lambda e: e.reciprocal(out=rbuf[:, 0:ncols], in_=rbuf[:, 0:ncols]), reads=[rbuf], writes=[rbuf])


def headnorm(k, ps_list, ncols, gcols, sumw, eps_buf, sq_bufs, ssp, rbuf, outs):
    for (ps, M), sq in zip(ps_list, sq_bufs):
        k.op("act", lambda e: e.activation(out=sq[0:M, 0:ncols], in_=ps[0:M, 0:ncols], func=AF.Square),
             reads=[ps], writes=[sq])
    k.deps("pe", list(sq_bufs[:len(ps_list)]) + [w for w in sumw], [ssp])
    for c0 in range(0, ncols, 512):
        n = min(512, ncols - c0)
        for j, ((ps, M), sq, ow) in enumerate(zip(ps_list, sq_bufs, sumw)):
            ins = k.e["pe"].matmul(ssp[:, c0:c0 + n], lhsT=ow[0:M, :], rhs=sq[0:M, c0:c0 + n],
                                   start=(j == 0), stop=(j == len(ps_list) - 1))
    ev = k.mark("pe", ins)
    k.done(ev, list(sq_bufs[:len(ps_list)]) + [w for w in sumw], [ssp])
    rstd_from(k, ssp, rbuf, ncols)


def build(C, seg=0):
    SEGMODE[0] = seg
    nc = bass.Bass("TRN2", target_bir_lowering=False)
    D, T, KC, AH, BH, DFF, NFB, QR, KVR, RPB = C.D, C.T, C.KC, C.AH, C.BH, C.DFF, C.NFB, C.QR, C.KVR, C.RPB
    AW = AH * 128
    NQKV = 9 * AW

    def din(name, shape, dt=F32):
        return nc.dram_tensor(name, list(shape), dt, kind="ExternalInput").ap()

    def dout(name, shape, dt=F32):
        return nc.dram_tensor(name, list(shape), dt, kind="ExternalOutput").ap()

    def dscr(name, shape, dt):
        return nc.dram_tensor(name, list(shape), dt).ap()

    need = lambda *segs: seg == 0 or seg in segs
    NGC = 6 + QR // 128 + KVR // 128 + 2
    gbc = din("gbc", [5, 128, D])
    gcol = din("gcol", [128, NGC])
    grope = din("grope", [64, 4])
    ident_in = din("ident", [128, 128])
    hv_in = din("hv", [128, 3])
    if need(1):
        x = din("x", [T, D])
        xh = din("xh", [2 * T, D])
        w_qkv = din("a_w_qkv", [D, NQKV])
        w_oa = din("a_w_o", [AW, D])
        t5tab = din("t5tab", [3 * AH, 128, 256])
        QT = dscr("QT", [3 * AH, 128, T], BF16)
        KT = [dscr("KT%d" % g, [AH, 128, C.HALO[g] + T], BF16) for g in range(3)]
        VV = [dscr("VV%d" % g, [C.HALO[g] + T, AW], BF16) for g in range(3)]
        OA = [dscr("OA%d" % g, [T, AH, 129], F32) for g in range(3)]
    if need(2, 4):
        w_up = {L: din("ffn_w_up%d" % L, [D, 2 * DFF]) for L in (0, 1) if need(2 + 2 * L)}
        w_dn = {L: din("ffn_w_down%d" % L, [DFF, D]) for L in (0, 1) if need(2 + 2 * L)}
        convw = din("convw", [128, 2, 3, 2 * NFB])
        convb = din("convb", [128, 2, 2 * NFB])
        sel_in = din("sel", [2, C.NCORE])
        hal_in = dscr("hal_in", [2, D], F32)
        hal_out = dscr("hal_out", [2 * C.NCORE, D], F32) if seg == 0 else din("hal_g", [2 * C.NCORE, D])
    if need(2, 3):
        cos_in = din("cos2", [64, T])
        sin_in = din("sin2", [64, T])
    if need(2):
        kv_wd = din("kv_w_down", [D, KVR + 64])
        kv_wu = din("kv_w_up", [KVR, BH * 256])
        mk = dscr if seg == 0 else (lambda n, sh, dt: dout(n, sh, dt))
        KBn_l = mk("KBn_l", [BH * 128, T], BF16)
        KBr_l = mk("KBr_l", [BH * 64, T], BF16)
        VB_l = mk("VB_l", [T, BH * 128], BF16)
    if need(3):
        w_dq = din("b_w_dq", [D, QR])
        w_uq = din("b_w_uq", [QR, BH * 192])
        w_ob = din("b_w_o", [BH * 128, D])
        rbL_in = din("rbL", [128, RPB])
        rbG_in = din("rbG", [128, RPB])
        tmix_in = din("tmix", [128, RPB * 4 * 512])
        QBn = dscr("QBn", [BH, 128, T], BF16)
        QBr = dscr("QBr", [BH, 64, T], BF16)
        mk = dscr if seg == 0 else (lambda n, sh, dt: din(n, sh, dt))
        KBn_g = mk("KBn_g", [RPB * BH * 128, T], BF16)
        KBr_g = mk("KBr_g", [RPB * BH * 64, T], BF16)
        VB_g = mk("VB_g", [RPB * T, BH * 128], BF16)
    out = dout("out", [T, D])
    hin = din("hin", [T, D]) if seg >= 2 else None

    gs = ExitStack()
    with gs:
        k = K(nc, gs)
        cc_sem = gs.enter_context(nc.semaphore("cc_sem"))
        cc_count = [0]

        def allgather(src, dst, groups):
            if seg != 0:
                return
            nc.gpsimd.collective_compute("AllGather", ALU.bypass, replica_groups=groups,
                                         ins=[src.opt()], outs=[dst.opt()]).then_inc(cc_sem)
            cc_count[0] += 1
            nc.gpsimd.wait_ge(cc_sem, cc_count[0])
            k.op("pool", lambda e: e.memset(ccjunk[:], 0.0), writes=[ccjunk])
            k.barrier()

        try:
          with Phase(k) as gph:
            ident = gph.sb([128, 128], BF16)
            idf = gph.sb([128, 128], F32, dma=True)
            gc = gph.sb([128, NGC], F32, dma=True)
            gr = gph.sb([64, 4], F32, dma=True)
            hv = gph.sb([128, 3], F32, dma=True)
            ones_dh = gph.sb([128, 128], BF16)
            ones_192 = gph.sb([128, 128], BF16)
            ones_qr = gph.sb([128, 128], BF16)
            ones_kv = gph.sb([128, 128], BF16)
            ones_1 = gph.sb([128, 128], BF16)
            ccjunk = gph.sb([128, 1], F32)
            epsc = gph.sb([128, 1], F32)
            EPSC[0] = epsc
            k.op("pool", lambda e: e.memset(epsc[:], EPS), writes=[epsc])
            k.dma("sp", idf[:], ident_in, writes=[idf])
            k.dma("sp", gc[:], gcol, writes=[gc])
            k.dma("sp", gr[:], grope, writes=[gr])
            k.dma("sp", hv[:], hv_in, writes=[hv])
            k.op("dve", lambda e: e.tensor_copy(out=ident[:], in_=idf[:]), reads=[idf], writes=[ident])
            for (ob, val) in ((ones_dh, 1.0 / 128), (ones_192, 1.0 / 192), (ones_qr, 1.0 / QR),
                              (ones_kv, 1.0 / KVR), (ones_1, 1.0)):
                k.op("pool", lambda e: e.memset(ob[:], val), writes=[ob])
            if seg >= 2:
                csem = k.get_dsem()
                for i in range(0, T, 128):
                    k.dma("sp", out[i:i + 128, :], hin[i:i + 128, :], sem=csem)
            k.barrier()

            if need(1):
                layer_a(k, C, nc, x, xh, w_qkv, w_oa, gbc, gc, hv, t5tab, ident, ones_dh, ones_1, QT, KT, VV, OA, out)
            checkpoint(3)
            if need(2):
                ffn(k, C, nc, 0, out, w_up, w_dn, gbc[3], convw, convb, sel_in, hal_in, hal_out, ident, allgather)
            checkpoint(4)
            if need(2):
                layer_b_kv(k, C, nc, out, kv_wd, kv_wu, gbc, gc, gr, cos_in, sin_in, ident, ones_192, ones_kv,
                           KBn_l, KBr_l, VB_l)
            if seg == 0:
                k.barrier()
                groups = [list(range(b * RPB, (b + 1) * RPB)) for b in range(C.BATCH)]
                allgather(KBn_l, KBn_g, groups)
                allgather(KBr_l, KBr_g, groups)
                allgather(VB_l, VB_g, groups)
            if need(3):
                layer_b_q(k, C, nc, out, w_dq, w_uq, w_ob, gbc, gc, gr, cos_in, sin_in, rbL_in, rbG_in, tmix_in,
                          ident, ones_192, ones_qr, ones_1, QBn, QBr, KBn_g, KBr_g, VB_g)
            checkpoint(5)
            if need(4):
                ffn(k, C, nc, 1, out, w_up, w_dn, gbc[4], convw, convb, sel_in, hal_in, hal_out, ident, allgather)
            k.barrier()
        except StopBuild:
            k.barrier()
    return nc


def layer_a(k, C, nc, x, xh, w_qkv, w_oa, gbc, gc, hv, t5tab, ident, ones_dh, ones_1, QT, KT, VV, OA, out):
    D, T, KC, AH = C.D, C.T, C.KC, C.AH
    AW = AH * 128
    scale = 128 ** -0.5
    for pss in range(3):
        with Phase(k) as ph:
            hnT = ph.sb([128, KC, T], BF16)
            gain = ph.sb([128, D], F32, dma=True)
            k.dma("sp", gain[:], gbc[0], writes=[gain])
            if pss == 0:
                src = lambda i: x[i * 128:(i + 1) * 128, :]
            elif pss == 1:
                src = lambda i: xh[T + i * 128:T + (i + 1) * 128, :]
            else:
                src = lambda i: xh[i * 128:(i + 1) * 128, :]
            norm_transpose(k, C, src, T // 128, gain, hnT, 0, ident)
            checkpoint(0.5)
            if pss == 0:
                rng = {0: (0, T), 1: (0, T), 2: (0, T)}
            elif pss == 1:
                rng = {0: (T - 128, 128), 1: (T - 512, 512), 2: (0, T)}
            else:
                rng = {2: (0, T)}
            with Phase(k) as p2:
                wring = [p2.sb([128, KC, 256], BF16, dma=True) for _ in range(3)]
                psring = [p2.ps([128, 1024]) for _ in range(2)]
                ssp = p2.ps([128, 1024])
                sq = p2.sb([128, 1024], BF16)
                rb = p2.sb([128, 1024], F32)
                stg = [p2.sb([128, 1024], BF16, dma=True) for _ in range(2)]
                nst = [0]

                def qk_epi_factory(c0, n):
                    def epi(tag, ps):
                        kind, g, h = tag
                        headnorm(k, [(ps, 128)], n, None, [ones_dh], None, [sq], ssp, rb, None)
                        st = stg[nst[0] % 2]
                        nst[0] += 1
                        gcoln = (0 if kind == "q" else 3) + g
                        k.op("dve", lambda e: e.scalar_tensor_tensor(out=st[:, 0:n], in0=ps[:, 0:n],
                                                                     scalar=gc[:, gcoln:gcoln + 1], in1=rb[:, 0:n],
                                                                     op0=ALU.mult, op1=ALU.mult),
                             reads=[ps, rb, gc], writes=[st])
                        if kind == "q":
                            dst = QT[g * AH + h, :, 0:n]
                        else:
                            base = C.HALO[g] + {0: 0, 1: -T, 2: -2 * T}[pss] + c0
                            dst = KT[g][h, :, base:base + n]
                        k.dma("sp", dst, st[:, 0:n], reads=[st])
                    return epi

                by_rng = {}
                for g, (c0, n) in rng.items():
                    by_rng.setdefault((c0, n), []).append(g)
                for (c0, n), gl in by_rng.items():
                    blocks = []
                    for g in gl:
                        kinds = ("q", "k") if pss == 0 else ("k",)
                        for kind in kinds:
                            base = (0 if kind == "q" else 3 * AW) + g * AW
                            for h in range(AH):
                                blocks.append((base + h * 128, (kind, g, h)))
                    items = []
                    for i in range(0, len(blocks), 2):
                        pc, bl = [], []
                        for j, (col, tag) in enumerate(blocks[i:i + 2]):
                            pc.append((w_qkv[:, col:col + 128], j * 128))
                            bl.append((j * 128, 128, tag))
                        if len(pc) == 2 and blocks[i + 1][0] == blocks[i][0] + 128:
                            pc = [(w_qkv[:, blocks[i][0]:blocks[i][0] + 256], 0)]
                        items.append(dict(pieces=pc, blocks=bl))
                    chunks = [(c0 + o, min(512, n - o)) for o in range(0, n, 512)]
                    gemm_b(k, p2, hnT, chunks, KC, items, qk_epi_factory(c0, n), wring, psring)
            checkpoint(0.6)
            with Phase(k) as p3:
                wring = [p3.sb([128, KC, 512], BF16, dma=True) for _ in range(2)]
                psring = [p3.ps([128, 512]) for _ in range(4)]
                stg = [p3.sb([128, 512], BF16, dma=True) for _ in range(3)]
                nst = [0]
                for g, (c0, n) in rng.items():
                    tts = [c0 + o for o in range(0, n, 128)]
                    items = []
                    for pc0 in range(0, AW, 512):
                        ncols = min(512, AW - pc0)
                        col = 6 * AW + g * AW + pc0
                        pieces = [(w_qkv[:, col + o:col + min(o + 256, ncols)], o) for o in range(0, ncols, 256)]
                        items.append(dict(pieces=pieces, ncols=ncols, tag=(g, pc0, ncols)))

                    def v_epi(tag, ti, ps, c0=c0):
                        g_, pc0, ncols = tag
                        st = stg[nst[0] % 3]
                        nst[0] += 1
                        if nst[0] % 2 == 0:
                            k.op("act", lambda e: e.copy(out=st[:, 0:ncols], in_=ps[:, 0:ncols]), reads=[ps], writes=[st])
                        else:
                            k.op("dve", lambda e: e.tensor_copy(out=st[:, 0:ncols], in_=ps[:, 0:ncols]), reads=[ps], writes=[st])
                        row = C.HALO[g_] + {0: 0, 1: -T, 2: -2 * T}[pss] + c0 + ti * 128
                        k.dma("sp", VV[g_][row:row + 128, pc0:pc0 + ncols], st[:, 0:ncols], reads=[st])

                    gemm_a(k, p3, hnT, tts, KC, items, v_epi, wring, psring)
            checkpoint(0.7)

    checkpoint(1)
    for g, (window, d) in enumerate(C.GROUPS):
        H = C.HALO[g]
        nq = T // d
        QB = min(128, nq)
        nblk = nq // QB
        with Phase(k) as ph:
            nkc = (128 + nq + 127) // 128
            vt = ph.sb([128, d * nkc, AW], BF16, dma=True)
            for r in range(d):
                for c in range(nkc):
                    nk = min(128, nq + 128 - 128 * c)
                    row0 = H + r + d * (-128 + 128 * c)
                    srcv = VV[g][row0:row0 + d * (nk - 1) + 1:d, :] if d > 1 else VV[g][row0:row0 + nk, :]
                    k.dma("sp", vt[0:nk, r * nkc + c, :], srcv, writes=[vt])
            kt = [ph.sb([128, H + T], BF16, dma=True) for _ in range(2)]
            qt = [ph.sb([128, T], BF16, dma=True) for _ in range(2)]
            tab = [ph.sb([128, 256], F32, dma=True) for _ in range(2)]
            tab0 = [ph.sb([128, 256], F32) for _ in range(2)]
            sps = [ph.ps([128, 256]) for _ in range(3)]
            ops = [ph.ps([128, 132]) for _ in range(2)]
            tmp = [ph.sb([128, 256], F32) for _ in range(2)]
            pt = [ph.sb([128, 256], BF16) for _ in range(3)]
            ost = [ph.sb([128, 132], F32, dma=True) for _ in range(3)]
            steps = [(h, r, qb) for h in range(AH) for r in range(d) for qb in range(nblk)]
            SKEW = 2

            def sl(t_, s0, n):
                return t_[:, s0:s0 + d * (n - 1) + 1:d] if d > 1 else t_[:, s0:s0 + n]

            def emit_S(i):
                h, r, qb = steps[i]
                kth, qth, tb, tb0 = kt[h % 2], qt[h % 2], tab[h % 2], tab0[h % 2]
                if r == 0 and qb == 0:
                    k.dma("sp", kth[:], KT[g][h], writes=[kth])
                    k.dma("sp", qth[:], QT[g * AH + h], writes=[qth])
                    k.dma("sp", tb[:], t5tab[g * AH + h], writes=[tb])
                    k.op("pool", lambda e: e.tensor_copy(out=tb0[:, 128:256], in_=tb[:, 128:256]), reads=[tb], writes=[tb0])
                    k.op("dve", lambda e: e.tensor_scalar(out=tb0[:, 0:128], in0=tb[:, 0:128], scalar1=hv[:, g:g + 1],
                                                          scalar2=None, op0=ALU.add), reads=[tb, hv, tb0], writes=[tb0])
                i0 = qb * QB
                sp_, tm, p_ = sps[i % 3], tmp[i % 2], pt[i % 3]
                tbu = tb0 if qb == 0 else tb
                ka0 = H + r + d * (i0 - 128)
                kb0 = H + r + d * i0
                q0 = r + d * i0
                k.deps("pe", [kth, qth], [sp_])
                k.e["pe"].matmul(sp_[:, 0:QB], lhsT=sl(kth, ka0, 128), rhs=sl(qth, q0, QB), start=True, stop=True)
                ins = k.e["pe"].matmul(sp_[0:QB, 128:128 + QB], lhsT=sl(kth, kb0, QB), rhs=sl(qth, q0, QB),
                                       start=True, stop=True)
                ev = k.mark("pe", ins)
                k.done(ev, [kth, qth], [sp_])
                if QB == 128:
                    k.op("dve", lambda e: e.scalar_tensor_tensor(out=tm[:, :], in0=sp_[:, :], scalar=scale,
                                                                 in1=tbu[:, :], op0=ALU.mult, op1=ALU.add),
                         reads=[sp_, tbu], writes=[tm])
                    k.op("act", lambda e: e.activation(out=p_[:, :], in_=tm[:, :], func=AF.Exp),
                         reads=[tm], writes=[p_])
                else:
                    k.op("dve", lambda e: e.scalar_tensor_tensor(out=tm[:, 0:QB], in0=sp_[:, 0:QB], scalar=scale,
                                                                 in1=tbu[:, 0:QB], op0=ALU.mult, op1=ALU.add),
                         reads=[sp_, tbu], writes=[tm])
                    k.op("dve", lambda e: e.scalar_tensor_tensor(out=tm[0:QB, 128:128 + QB], in0=sp_[0:QB, 128:128 + QB],
                                                                 scalar=scale, in1=tbu[0:QB, 128:128 + QB],
                                                                 op0=ALU.mult, op1=ALU.add),
                         reads=[sp_, tbu, tm], writes=[tm])
                    k.op("act", lambda e: e.activation(out=p_[:, 0:QB], in_=tm[:, 0:QB], func=AF.Exp),
                         reads=[tm], writes=[p_])
                    k.op("act", lambda e: e.activation(out=p_[0:QB, 128:128 + QB], in_=tm[0:QB, 128:128 + QB], func=AF.Exp),
                         reads=[tm, p_], writes=[p_])

            def emit_PV(i):
                h, r, qb = steps[i]
                i0 = qb * QB
                p_, op_, os_ = pt[i % 3], ops[i % 2], ost[i % 3]
                ca = (i0 // 128) if QB == 128 else 0
                va = vt[:, r * nkc + ca, h * 128:(h + 1) * 128]
                vb = vt[0:QB, r * nkc + ca + 1, h * 128:(h + 1) * 128]
                k.deps("pe", [p_, vt, ones_1], [op_])
                k.e["pe"].matmul(op_[0:QB, 0:128], lhsT=p_[:, 0:QB], rhs=va, start=True, stop=False)
                k.e["pe"].matmul(op_[0:QB, 0:128], lhsT=p_[0:QB, 128:128 + QB], rhs=vb, start=False, stop=True)
                k.e["pe"].matmul(op_[0:QB, 128:129], lhsT=p_[:, 0:QB], rhs=ones_1[:, 0:1], start=True, stop=False)
                ins = k.e["pe"].matmul(op_[0:QB, 128:129], lhsT=p_[0:QB, 128:128 + QB], rhs=ones_1[0:QB, 0:1],
                                       start=False, stop=True)
                ev = k.mark("pe", ins)
                k.done(ev, [p_, vt, ones_1], [op_])
                k.op("act", lambda e: e.copy(out=os_[0:QB, 0:129], in_=op_[0:QB, 0:129]), reads=[op_], writes=[os_])
                t0 = r + d * i0
                dsto = OA[g][t0:t0 + d * (QB - 1) + 1:d, h, :] if d > 1 else OA[g][t0:t0 + QB, h, :]
                k.dma("sp", dsto, os_[0:QB, 0:129], reads=[os_])

            for i in range(len(steps) + SKEW):
                if i < len(steps):
                    emit_S(i)
                if i >= SKEW:
                    emit_PV(i - SKEW)

    checkpoint(2)
    with Phase(k) as ph:
        oT = ph.sb([128, AH, T], BF16)
        with Phase(k) as p2:
            oa = [[p2.sb([128, AH, 129], F32, dma=True) for _ in range(3)] for _ in range(2)]
            rl = [p2.sb([128, AH, 1], F32) for _ in range(2)]
            ob = [p2.sb([128, AH, 128], BF16) for _ in range(2)]
            tp = [p2.ps([128, 8, 128], BF16) for _ in range(2)]
            ng = 0
            for tt in range(T // 128):
                o3 = oa[tt % 2]
                for g in range(3):
                    k.dma("sp", o3[g][:], OA[g][tt * 128:(tt + 1) * 128], writes=[o3[g]])
                k.op("dve", lambda e: e.tensor_tensor(out=o3[0][:], in0=o3[0][:], in1=o3[1][:], op=ALU.add),
                     reads=[o3[0], o3[1]], writes=[o3[0]])
                k.op("dve", lambda e: e.tensor_tensor(out=o3[0][:], in0=o3[0][:], in1=o3[2][:], op=ALU.add),
                     reads=[o3[0], o3[2]], writes=[o3[0]])
                rli, obi = rl[tt % 2], ob[tt % 2]
                k.op("dve", lambda e: e.reciprocal(out=rli[:], in_=o3[0][:, :, 128:129]), reads=[o3[0]], writes=[rli])
                for h in range(AH):
                    eng = "act" if h % 2 == 0 else "dve"
                    if eng == "act":
                        k.op("act", lambda e: e.activation(out=obi[:, h, :], in_=o3[0][:, h, 0:128], func=AF.Copy,
                                                           scale=rli[:, h, 0:1]), reads=[o3[0], rli], writes=[obi])
                    else:
                        k.op("dve", lambda e: e.tensor_scalar(out=obi[:, h, :], in0=o3[0][:, h, 0:128],
                                                              scalar1=rli[:, h, 0:1], scalar2=None, op0=ALU.mult),
                             reads=[o3[0], rli], writes=[obi])
                for g0 in range(0, AH, 8):
                    n = min(8, AH - g0)
                    tpj = tp[ng % 2]
                    ng += 1
                    k.deps("pe", [obi, ident], [tpj])
                    for c in range(n):
                        ins = k.e["pe"].transpose(out=tpj[:, c, :], in_=obi[:, g0 + c, :], identity=ident[:])
                    ev = k.mark("pe", ins)
                    k.done(ev, [obi, ident], [tpj])
                    k.op("act", lambda e: e.copy(out=oT[:, g0:g0 + n, tt * 128:(tt + 1) * 128], in_=tpj[:, 0:n, :]),
                         reads=[tpj], writes=[oT])
        checkpoint(2.5)
        proj_residual(k, C, oT, AH, w_oa, x, out)


def proj_residual(k, C, actT, KCn, w, res_src, out):
    D, T = C.D, C.T
    with Phase(k) as p3:
        wring = [p3.sb([128, KCn, 512], BF16, dma=True) for _ in range(2)]
        psring = [p3.ps([128, 512]) for _ in range(4)]
        rs = [p3.sb([128, 512], F32, dma=True) for _ in range(3)]
        nst = [0]
        items = []
        for pc0 in range(0, D, 512):
            pieces = [(w[:, pc0 + o:pc0 + o + 256], o) for o in range(0, 512, 256)]
            items.append(dict(pieces=pieces, ncols=512, tag=pc0))

        def epi(pc0, ti, ps):
            r_ = rs[nst[0] % 3]
            nst[0] += 1
            k.dma("sp", r_[:], res_src[ti * 128:(ti + 1) * 128, pc0:pc0 + 512], writes=[r_])
            k.op("dve", lambda e: e.tensor_tensor(out=r_[:], in0=ps[:, 0:512], in1=r_[:], op=ALU.add),
                 reads=[ps, r_], writes=[r_])
            k.dma("sp", out[ti * 128:(ti + 1) * 128, pc0:pc0 + 512], r_[:], reads=[r_])

        gemm_a(k, p3, actT, [i * 128 for i in range(T // 128)], KCn, items, epi, wring, psring)


def ffn(k, C, nc, L, out, w_up, w_dn, gain_src, convw, convb, sel_in, hal_in, hal_out, ident, allgather):
    D, T, KC, DFF, NFB, NF = C.D, C.T, C.KC, C.DFF, C.NFB, C.NF
    W = T + 2
    k.barrier()
    if hal_in is not None and SEGMODE[0] == 0:
        hsem = k.get_dsem()
        k.dma("pool", hal_in, out[T - 2:T, :], sem=hsem)
        k.barrier()
        allgather(hal_in, hal_out, [list(range(C.NCORE))])
        k.put_dsem(hsem)
    with Phase(k) as ph:
        hnT = ph.sb([128, KC, W], BF16)
        gain = ph.sb([128, D], F32, dma=True)
        k.dma("sp", gain[:], gain_src, writes=[gain])
        with Phase(k) as p1:
            NCR = C.NCORE
            HC = min(D, 512)
            sel = p1.sb([128, NCR], F32, dma=True)
            htile = p1.sb([128, D], F32)
            k.dma("sp", sel[0:2, :], sel_in, writes=[sel])
            k.op("pool", lambda e: e.memset(htile[:], 0.0), writes=[htile])
            halv = hal_out.rearrange("(r i) d -> i r d", i=2)
            with Phase(k) as p1a:
                hgs = [p1a.sb([128, NCR, HC], F32, dma=True) for _ in range(2)]
                for ci, c0 in enumerate(range(0, D, HC)):
                    hg = hgs[ci % 2]
                    k.dma("pool", hg[0:2, :, :], halv[:, :, c0:c0 + HC], writes=[hg])
                    for r_ in range(NCR):
                        k.op("dve", lambda e: e.scalar_tensor_tensor(out=htile[0:2, c0:c0 + HC], in0=hg[0:2, r_, :],
                                                                     scalar=sel[0:2, r_:r_ + 1], in1=htile[0:2, c0:c0 + HC],
                                                                     op0=ALU.mult, op1=ALU.add),
                             reads=[hg, sel, htile], writes=[htile])
            checkpoint(3.3)
            with Phase(k) as p1b:
                hT = p1b.sb([128, KC, 128], BF16)
                norm_transpose(k, C, None, 1, gain, hT, 0, ident, pre_tile=htile)
                k.op("dve", lambda e: e.tensor_copy(out=hnT[:, :, 0:2], in_=hT[:, :, 0:2]), reads=[hT], writes=[hnT])
        norm_transpose(k, C, lambda i: out[i * 128:(i + 1) * 128, :], T // 128, gain, hnT, 2, ident)

        checkpoint(3.4)
        cw = ph.sb([128, 3, 2 * NFB], F32, dma=True)
        cb = ph.sb([128, 2 * NFB], F32, dma=True)
        k.dma("sp", cw[:], convw[:, L], writes=[cw])
        k.dma("sp", cb[:], convb[:, L], writes=[cb])
        fsplit = [NFB * i // NF for i in range(NF + 1)]
        for qf in range(NF):
            f0, f1 = fsplit[qf], fsplit[qf + 1]
            nfq = f1 - f0
            with Phase(k) as pq:
                GT = pq.sb([128, nfq, T], BF16)
                with Phase(k) as p2:
                    wring = [p2.sb([128, KC, 256], BF16, dma=True) for _ in range(2)]
                    psring = [p2.ps([128, 1536]) for _ in range(2)]
                    ga = [p2.sb([128, T], F32) for _ in range(2)]
                    ua = [p2.sb([128, T], F32) for _ in range(2)]
                    gsl = [p2.sb([128, T], F32) for _ in range(2)]
                    items = []
                    for f in range(f0, f1):
                        items.append(dict(pieces=[(w_up[L][:, f * 128:(f + 1) * 128], 0),
                                                  (w_up[L][:, DFF + f * 128:DFF + (f + 1) * 128], 128)],
                                          blocks=[(0, 128, ("g", f)), (128, 128, ("u", f))]))

                    def up_epi(tag, ps):
                        kind, f = tag
                        col = f if kind == "g" else NFB + f
                        a = (ga if kind == "g" else ua)[f % 2]
                        k.op("act", lambda e: e.activation(out=a[:], in_=ps[:, 2:T + 2], func=AF.Identity,
                                                           bias=cb[:, col:col + 1], scale=cw[:, 2, col:col + 1]),
                             reads=[ps, cb, cw], writes=[a])
                        k.op("dve", lambda e: e.scalar_tensor_tensor(out=a[:], in0=ps[:, 1:T + 1], scalar=cw[:, 1, col:col + 1],
                                                                     in1=a[:], op0=ALU.mult, op1=ALU.add),
                             reads=[ps, cw, a], writes=[a])
                        k.op("dve", lambda e: e.scalar_tensor_tensor(out=a[:], in0=ps[:, 0:T], scalar=cw[:, 0, col:col + 1],
                                                                     in1=a[:], op0=ALU.mult, op1=ALU.add),
                             reads=[ps, cw, a], writes=[a])
                        if kind == "g":
                            s_ = gsl[f % 2]
                            k.op("act", lambda e: e.activation(out=s_[:], in_=a[:], func=AF.Silu), reads=[a], writes=[s_])
                        else:
                            s_ = gsl[f % 2]
                            k.op("pool", lambda e: e.tensor_tensor(out=GT[:, f - f0, :], in0=s_[:], in1=a[:], op=ALU.mult),
                                 reads=[s_, a], writes=[GT])

                    chunks = [(0, 512), (512, 512), (1024, 2)]
                    gemm_b(k, p2, hnT, chunks, KC, items, up_epi, wring, psring)
                checkpoint(3.5)
                with Phase(k) as p3:
                    wring = [p3.sb([128, nfq, 512], BF16, dma=True) for _ in range(2)]
                    psring = [p3.ps([128, 512]) for _ in range(4)]
                    rs = [p3.sb([128, 512], F32, dma=True) for _ in range(3)]
                    nst = [0]
                    items = []
                    for pc0 in range(0, D, 512):
                        pieces = [(w_dn[L][f0 * 128:f1 * 128, pc0 + o:pc0 + o + 256], o) for o in range(0, 512, 256)]
                        items.append(dict(pieces=pieces, ncols=512, tag=pc0))

                    def dn_epi(pc0, ti, ps):
                        r_ = rs[nst[0] % 3]
                        nst[0] += 1
                        k.dma("sp", r_[:], out[ti * 128:(ti + 1) * 128, pc0:pc0 + 512], writes=[r_])
                        k.op("dve", lambda e: e.tensor_tensor(out=r_[:], in0=ps[:, 0:512], in1=r_[:], op=ALU.add),
                             reads=[ps, r_], writes=[r_])
                        k.dma("sp", out[ti * 128:(ti + 1) * 128, pc0:pc0 + 512], r_[:], reads=[r_])

                    gemm_a(k, p3, GT, [i * 128 for i in range(T // 128)], nfq, items, dn_epi, wring, psring)


def layer_b_kv(k, C, nc, out, kv_wd, kv_wu, gbc, gc, gr, cos_in, sin_in, ident, ones_192, ones_kv, KBn_l, KBr_l, VB_l):
    D, T, KC, BH, QR, KVR, RPB = C.D, C.T, C.KC, C.BH, C.QR, C.KVR, C.RPB
    QC, KVC = QR // 128, KVR // 128
    scale = 192 ** -0.5
    GQL, GKL, GQN, GKN = 6, 6 + QC, 6 + QC + KVC, 6 + QC + KVC + 1
    with Phase(k) as ph:
        tC = [None, None]
        tS = [None, None]
        i = 1
        tC[i] = ph.sb([64, T], F32, dma=True)
        tS[i] = ph.sb([64, T], F32, dma=True)
        k.dma("sp", tC[i][:], cos_in, writes=[tC[i]])
        k.dma("sp", tS[i][:], sin_in, writes=[tS[i]])
        k.op("dve", lambda e: e.tensor_scalar(out=tC[i][:], in0=tC[i][:], scalar1=gr[:, 2 * i:2 * i + 1], scalar2=None,
                                              op0=ALU.mult), reads=[tC[i], gr], writes=[tC[i]])
        k.op("dve", lambda e: e.tensor_scalar(out=tS[i][:], in0=tS[i][:], scalar1=gr[:, 2 * i + 1:2 * i + 2], scalar2=None,
                                              op0=ALU.mult), reads=[tS[i], gr], writes=[tS[i]])
        sq = [ph.sb([128, T], BF16) for _ in range(1)]
        rb = ph.sb([128, T], F32)
        ckvT = ph.sb([128, KVC, T], BF16)
        krope = ph.sb([64, T], F32)
        sqkr = ph.sb([64, T], BF16)
        with Phase(k) as p1:
            hnT = p1.sb([128, KC, T], BF16)
            gain = p1.sb([128, D], F32, dma=True)
            k.dma("sp", gain[:], gbc[1], writes=[gain])
            norm_transpose(k, C, lambda i: out[i * 128:(i + 1) * 128, :], T // 128, gain, hnT, 0, ident)
            with Phase(k) as p2:
                wring = [p2.sb([128, KC, 256], BF16, dma=True) for _ in range(3)]
                psring = [p2.ps([128, 1024]) for _ in range(2)]
                ssp = p2.ps([128, 1024])
                raw = p2.sb([128, KVC, T], F32)
                kr_raw = p2.sb([64, T], F32)
                kr_sw = p2.sb([64, T], F32)
                items = []
                for c in range(0, KVC, 2):
                    n = min(2, KVC - c)
                    items.append(dict(pieces=[(kv_wd[:, c * 128:(c + n) * 128], 0)],
                                      blocks=[(j * 128, 128, ("c", c + j)) for j in range(n)]))
                items.append(dict(pieces=[(kv_wd[:, KVR:KVR + 64], 0), (kv_wd[:, KVR + 32:KVR + 64], 64),
                                          (kv_wd[:, KVR:KVR + 32], 96)],
                                  blocks=[(0, 64, ("r", 0)), (64, 64, ("s", 0))]))

                def kvd_epi(tag, ps):
                    kind, c = tag
                    if kind == "c":
                        k.op("act", lambda e: e.copy(out=raw[:, c, :], in_=ps[:, 0:T]), reads=[ps], writes=[raw])
                    elif kind == "r":
                        k.op("act", lambda e: e.copy(out=kr_raw[:], in_=ps[0:64, 0:T]), reads=[ps], writes=[kr_raw])
                    else:
                        k.op("act", lambda e: e.copy(out=kr_sw[:], in_=ps[0:64, 0:T]), reads=[ps], writes=[kr_sw])

                gemm_b(k, p2, hnT, [(0, 512), (512, 512)], KC, items, kvd_epi, wring, psring)
                sqc = [p2.sb([128, T], BF16) for _ in range(2)]
                for c in range(KVC):
                    sqb = sqc[c % 2]
                    k.op("act", lambda e: e.activation(out=sqb[:], in_=raw[:, c, :], func=AF.Square),
                         reads=[raw], writes=[sqb])
                    k.deps("pe", [sqb, ones_kv], [ssp] if c == 0 else [])
                    for c0 in range(0, T, 512):
                        ins = k.e["pe"].matmul(ssp[:, c0:c0 + 512], lhsT=ones_kv[:, :], rhs=sqb[:, c0:c0 + 512],
                                               start=(c == 0), stop=(c == KVC - 1))
                    ev = k.mark("pe", ins)
                    k.done(ev, [sqb, ones_kv], [])
                ssp.w = ev
                ssp.r = []
                rstd_from(k, ssp, rb, T)
                for c in range(KVC):
                    k.op("dve", lambda e: e.scalar_tensor_tensor(out=ckvT[:, c, :], in0=raw[:, c, :],
                                                                 scalar=gc[:, GKL + c:GKL + c + 1], in1=rb[:],
                                                                 op0=ALU.mult, op1=ALU.mult),
                         reads=[raw, gc, rb], writes=[ckvT])
                k.op("act", lambda e: e.activation(out=sqkr[:], in_=kr_raw[:], func=AF.Square), reads=[kr_raw], writes=[sqkr])
                k.op("dve", lambda e: e.tensor_tensor(out=krope[:], in0=kr_raw[:], in1=tC[1][:], op=ALU.mult),
                     reads=[kr_raw, tC[1]], writes=[krope])
                k.op("dve", lambda e: e.tensor_tensor(out=kr_sw[:], in0=kr_sw[:], in1=tS[1][:], op=ALU.mult),
                     reads=[kr_sw, tS[1]], writes=[kr_sw])
                k.op("dve", lambda e: e.tensor_tensor(out=krope[:], in0=krope[:], in1=kr_sw[:], op=ALU.add),
                     reads=[krope, kr_sw], writes=[krope])

        with Phase(k) as p2:
            wring = [p2.sb([128, KVC, 256], BF16, dma=True) for _ in range(3)]
            psring = [p2.ps([128, 1024]) for _ in range(2)]
            ssp = p2.ps([128, 1024])
            stn = [p2.sb([128, T], BF16, dma=True) for _ in range(2)]
            strp = [p2.sb([64, T], BF16, dma=True) for _ in range(2)]
            items = []
            for h in range(0, BH, 2):
                items.append(dict(pieces=[(kv_wu[:, h * 256:h * 256 + 128], 0), (kv_wu[:, (h + 1) * 256:(h + 1) * 256 + 128], 128)],
                                  blocks=[(0, 128, h), (128, 128, h + 1)]))

            def k_epi(h, ps):
                k.op("act", lambda e: e.activation(out=sq[0][:], in_=ps[:, 0:T], func=AF.Square), reads=[ps], writes=[sq[0]])
                k.deps("pe", [sq[0], sqkr, ones_192], [ssp])
                for c0 in range(0, T, 512):
                    k.e["pe"].matmul(ssp[:, c0:c0 + 512], lhsT=ones_192[:, :], rhs=sq[0][:, c0:c0 + 512], start=True, stop=False)
                    ins = k.e["pe"].matmul(ssp[:, c0:c0 + 512], lhsT=ones_192[0:64, :], rhs=sqkr[:, c0:c0 + 512],
                                           start=False, stop=True)
                ev = k.mark("pe", ins)
                k.done(ev, [sq[0], sqkr, ones_192], [ssp])
                rstd_from(k, ssp, rb, T)
                sn, sr = stn[h % 2], strp[h % 2]
                k.op("dve", lambda e: e.scalar_tensor_tensor(out=sn[:], in0=ps[:, 0:T], scalar=gc[:, GKN:GKN + 1], in1=rb[:],
                                                             op0=ALU.mult, op1=ALU.mult), reads=[ps, gc, rb], writes=[sn])
                k.op("pool", lambda e: e.tensor_tensor(out=sr[:], in0=krope[:], in1=rb[0:64, :], op=ALU.mult),
                     reads=[krope, rb], writes=[sr])
                k.dma("sp", KBn_l[h * 128:(h + 1) * 128, :], sn[:], reads=[sn])
                k.dma("sp", KBr_l[h * 64:(h + 1) * 64, :], sr[:], reads=[sr])

            gemm_b(k, p2, ckvT, [(0, 512), (512, 512)], KVC, items, k_epi, wring, psring)
        with Phase(k) as p3:
            wring = [p3.sb([128, KVC, 512], BF16, dma=True) for _ in range(2)]
            psring = [p3.ps([128, 512]) for _ in range(4)]
            stg = [p3.sb([128, 512], BF16, dma=True) for _ in range(3)]
            nst = [0]
            items = []
            for h0 in range(0, BH, 4):
                nh = min(4, BH - h0)
                items.append(dict(pieces=[(kv_wu[:, (h0 + j) * 256 + 128:(h0 + j) * 256 + 256], j * 128) for j in range(nh)],
                                  ncols=nh * 128, tag=(h0, nh)))

            def vb_epi(tag, ti, ps):
                h0, nh = tag
                st = stg[nst[0] % 3]
                nst[0] += 1
                k.op("act", lambda e: e.copy(out=st[:, 0:nh * 128], in_=ps[:, 0:nh * 128]), reads=[ps], writes=[st])
                k.dma("sp", VB_l[ti * 128:(ti + 1) * 128, h0 * 128:(h0 + nh) * 128], st[:, 0:nh * 128], reads=[st])

            gemm_a(k, p3, ckvT, [i * 128 for i in range(T // 128)], KVC, items, vb_epi, wring, psring)


def layer_b_q(k, C, nc, out, w_dq, w_uq, w_ob, gbc, gc, gr, cos_in, sin_in, rbL_in, rbG_in, tmix_in,
              ident, ones_192, ones_qr, ones_1, QBn, QBr, KBn_g, KBr_g, VB_g):
    D, T, KC, BH, QR, KVR, RPB = C.D, C.T, C.KC, C.BH, C.QR, C.KVR, C.RPB
    QC, KVC = QR // 128, KVR // 128
    scale = 192 ** -0.5
    GQL, GKL, GQN, GKN = 6, 6 + QC, 6 + QC + KVC, 6 + QC + KVC + 1
    with Phase(k) as ph:
        tC = [None, None]
        tS = [None, None]
        i = 0
        tC[i] = ph.sb([64, T], F32, dma=True)
        tS[i] = ph.sb([64, T], F32, dma=True)
        k.dma("sp", tC[i][:], cos_in, writes=[tC[i]])
        k.dma("sp", tS[i][:], sin_in, writes=[tS[i]])
        k.op("dve", lambda e: e.tensor_scalar(out=tC[i][:], in0=tC[i][:], scalar1=gr[:, 2 * i:2 * i + 1], scalar2=None,
                                              op0=ALU.mult), reads=[tC[i], gr], writes=[tC[i]])
        k.op("dve", lambda e: e.tensor_scalar(out=tS[i][:], in0=tS[i][:], scalar1=gr[:, 2 * i + 1:2 * i + 2], scalar2=None,
                                              op0=ALU.mult), reads=[tS[i], gr], writes=[tS[i]])
        sq = [ph.sb([128, T], BF16) for _ in range(2)]
        rb = ph.sb([128, T], F32)
        cqT = ph.sb([128, QC, T], BF16)
        with Phase(k) as p1:
            hnT = p1.sb([128, KC, T], BF16)
            gain = p1.sb([128, D], F32, dma=True)
            k.dma("sp", gain[:], gbc[2], writes=[gain])
            norm_transpose(k, C, lambda i: out[i * 128:(i + 1) * 128, :], T // 128, gain, hnT, 0, ident)
            with Phase(k) as p2:
                wring = [p2.sb([128, KC, 256], BF16, dma=True) for _ in range(2)]
                psring = [p2.ps([128, 1024]) for _ in range(2)]
                ssp = p2.ps([128, 1024])
                raw = p2.sb([128, QC, T], F32)
                items = []
                for c in range(0, QC, 2):
                    n = min(2, QC - c)
                    items.append(dict(pieces=[(w_dq[:, c * 128:(c + n) * 128], 0)],
                                      blocks=[(j * 128, 128, c + j) for j in range(n)]))

                def dq_epi(c, ps):
                    k.op("act", lambda e: e.copy(out=raw[:, c, :], in_=ps[:, 0:T]), reads=[ps], writes=[raw])

                gemm_b(k, p2, hnT, [(0, 512), (512, 512)], KC, items, dq_epi, wring, psring)
                sqc = [p2.sb([128, T], BF16) for _ in range(2)]
                for c in range(QC):
                    sqb = sqc[c % 2]
                    k.op("act", lambda e: e.activation(out=sqb[:], in_=raw[:, c, :], func=AF.Square),
                         reads=[raw], writes=[sqb])
                    k.deps("pe", [sqb, ones_qr], [ssp] if c == 0 else [])
                    for c0 in range(0, T, 512):
                        ins = k.e["pe"].matmul(ssp[:, c0:c0 + 512], lhsT=ones_qr[:, :], rhs=sqb[:, c0:c0 + 512],
                                               start=(c == 0), stop=(c == QC - 1))
                    ev = k.mark("pe", ins)
                    k.done(ev, [sqb, ones_qr], [])
                ssp.w = ev
                ssp.r = []
                rstd_from(k, ssp, rb, T)
                for c in range(QC):
                    k.op("dve", lambda e: e.scalar_tensor_tensor(out=cqT[:, c, :], in0=raw[:, c, :],
                                                                 scalar=gc[:, GQL + c:GQL + c + 1], in1=rb[:],
                                                                 op0=ALU.mult, op1=ALU.mult),
                         reads=[raw, gc, rb], writes=[cqT])
        with Phase(k) as p2:
            wring = [p2.sb([128, QC, 256], BF16, dma=True) for _ in range(3)]
            psring = [p2.ps([128, 1024]) for _ in range(3)]
            ssp = p2.ps([128, 1024])
            stn = [p2.sb([128, T], BF16, dma=True) for _ in range(2)]
            strp = [p2.sb([64, T], BF16, dma=True) for _ in range(2)]
            ra = p2.sb([64, T], F32)
            rbb = p2.sb([64, T], F32)
            items = []
            for h in range(BH):
                c = h * 192
                items.append(dict(pieces=[(w_uq[:, c:c + 192], 0), (w_uq[:, c + 160:c + 192], 192), (w_uq[:, c + 128:c + 160], 224)],
                                  blocks=[(0, 128, ("n", h)), (128, 64, ("r", h)), (192, 64, ("s", h))]))
            hold = {}

            def uq_epi(tag, ps):
                kind, h = tag
                hold[kind] = ps
                if kind != "s":
                    return
                pn, pr, psw = hold["n"], hold["r"], hold["s"]
                k.op("act", lambda e: e.activation(out=sq[0][:], in_=pn[:, 0:T], func=AF.Square), reads=[pn], writes=[sq[0]])
                k.op("act", lambda e: e.activation(out=sq[1][0:64, :], in_=pr[0:64, 0:T], func=AF.Square), reads=[pr], writes=[sq[1]])
                k.deps("pe", [sq[0], sq[1], ones_192], [ssp])
                for c0 in range(0, T, 512):
                    k.e["pe"].matmul(ssp[:, c0:c0 + 512], lhsT=ones_192[:, :], rhs=sq[0][:, c0:c0 + 512], start=True, stop=False)
                    ins = k.e["pe"].matmul(ssp[:, c0:c0 + 512], lhsT=ones_192[0:64, :], rhs=sq[1][0:64, c0:c0 + 512],
                                           start=False, stop=True)
                ev = k.mark("pe", ins)
                k.done(ev, [sq[0], sq[1], ones_192], [ssp])
                rstd_from(k, ssp, rb, T)
                sn, sr = stn[h % 2], strp[h % 2]
                k.op("dve", lambda e: e.scalar_tensor_tensor(out=sn[:], in0=pn[:, 0:T], scalar=gc[:, GQN:GQN + 1], in1=rb[:],
                                                             op0=ALU.mult, op1=ALU.mult), reads=[pn, gc, rb], writes=[sn])
                k.op("dve", lambda e: e.tensor_tensor(out=ra[:], in0=pr[0:64, 0:T], in1=tC[0][:], op=ALU.mult),
                     reads=[pr, tC[0]], writes=[ra])
                k.op("dve", lambda e: e.tensor_tensor(out=rbb[:], in0=psw[0:64, 0:T], in1=tS[0][:], op=ALU.mult),
                     reads=[psw, tS[0]], writes=[rbb])
                k.op("pool", lambda e: e.tensor_tensor(out=ra[:], in0=ra[:], in1=rbb[:], op=ALU.add), reads=[ra, rbb], writes=[ra])
                k.op("pool", lambda e: e.tensor_tensor(out=sr[:], in0=ra[:], in1=rb[0:64, :], op=ALU.mult),
                     reads=[ra, rb], writes=[sr])
                k.dma("sp", QBn[h], sn[:], reads=[sn])
                k.dma("sp", QBr[h], sr[:], reads=[sr])

            gemm_b(k, p2, cqT, [(0, 512), (512, 512)], QC, items, uq_epi, wring, psring)

    with Phase(k) as ph:
        aT = ph.sb([128, BH, T], BF16)
        with Phase(k) as p2:
            rbL = p2.sb([128, RPB], F32, dma=True)
            rbG = p2.sb([128, RPB], F32, dma=True)
            tmix = p2.sb([128, RPB * 4 * 512], F32, dma=True)
            k.dma("sp", rbL[:], rbL_in, writes=[rbL])
            k.dma("sp", rbG[:], rbG_in, writes=[rbG])
            k.dma("sp", tmix[:], tmix_in, writes=[tmix])
            NKT = RPB * (T // 128)
            ktn = [p2.sb([128, RPB * T], BF16, dma=True) for _ in range(2)]
            ktr = [p2.sb([64, RPB * T], BF16, dma=True) for _ in range(2)]
            vth = [p2.sb([128, NKT, 128], BF16, dma=True) for _ in range(2)]
            qn = [p2.sb([128, T], BF16, dma=True) for _ in range(2)]
            qr = [p2.sb([64, T], BF16, dma=True) for _ in range(2)]
            sps = [p2.ps([128, 512]) for _ in range(3)]
            ops = [p2.ps([128, 512]) for _ in range(2)]
            lps = [p2.ps([128, 512]) for _ in range(2)]
            tm = [p2.sb([128, 512], F32) for _ in range(2)]
            pt = [p2.sb([128, 512], BF16) for _ in range(3)]
            rl = [p2.sb([128, 512], F32) for _ in range(2)]
            NQC = T // 512
            steps = [(h, qc, kt) for h in range(BH) for qc in range(NQC) for kt in range(NKT)]
            SKEW = 2
            last_ev = {}

            def emit_S(i):
                h, qc, kt = steps[i]
                a, b_, v_, qn_, qr_ = ktn[h % 2], ktr[h % 2], vth[h % 2], qn[h % 2], qr[h % 2]
                if qc == 0 and kt == 0:
                    for kr in range(RPB):
                        k.dma("sp", a[:, kr * T:(kr + 1) * T], KBn_g[(kr * BH + h) * 128:(kr * BH + h + 1) * 128, :], writes=[a])
                        k.dma("sp", b_[:, kr * T:(kr + 1) * T], KBr_g[(kr * BH + h) * 64:(kr * BH + h + 1) * 64, :], writes=[b_])
                    k.dma("sp", v_[:], VB_g[:, h * 128:(h + 1) * 128].rearrange("(c p) n -> p c n", p=128), writes=[v_])
                    k.dma("sp", qn_[:], QBn[h], writes=[qn_])
                    k.dma("sp", qr_[:], QBr[h], writes=[qr_])
                kr, ktl = kt // (T // 128), kt % (T // 128)
                sp_, p_, tm_ = sps[i % 3], pt[i % 3], tm[i % 2]
                k.deps("pe", [a, b_, qn_, qr_], [sp_])
                k.e["pe"].matmul(sp_[:, :], lhsT=a[:, kt * 128:(kt + 1) * 128], rhs=qn_[:, qc * 512:(qc + 1) * 512],
                                 start=True, stop=False)
                ins = k.e["pe"].matmul(sp_[:, :], lhsT=b_[:, kt * 128:(kt + 1) * 128], rhs=qr_[:, qc * 512:(qc + 1) * 512],
                                       start=False, stop=True)
                ev = k.mark("pe", ins)
                k.done(ev, [a, b_, qn_, qr_], [sp_])
                rel = ktl - 4 * qc
                if rel < 0:
                    k.op("act", lambda e: e.activation(out=p_[:], in_=sp_[:], func=AF.Exp, bias=rbL[:, kr:kr + 1], scale=scale),
                         reads=[sp_, rbL], writes=[p_])
                elif rel > 3:
                    k.op("act", lambda e: e.activation(out=p_[:], in_=sp_[:], func=AF.Exp, bias=rbG[:, kr:kr + 1], scale=scale),
                         reads=[sp_, rbG], writes=[p_])
                else:
                    o_ = (kr * 4 + rel) * 512
                    k.op("dve", lambda e: e.scalar_tensor_tensor(out=tm_[:], in0=sp_[:], scalar=scale, in1=tmix[:, o_:o_ + 512],
                                                                 op0=ALU.mult, op1=ALU.add), reads=[sp_, tmix], writes=[tm_])
                    k.op("act", lambda e: e.activation(out=p_[:], in_=tm_[:], func=AF.Exp), reads=[tm_], writes=[p_])

            def emit_PV(i):
                h, qc, kt = steps[i]
                v_ = vth[h % 2]
                p_ = pt[i % 3]
                g_ = h * NQC + qc
                op_, lp_, rl_ = ops[g_ % 2], lps[g_ % 2], rl[g_ % 2]
                if kt == 0:
                    k.deps("pe", [], [op_, lp_])
                k.deps("pe", [p_, v_, ones_1], [])
                k.e["pe"].matmul(op_[:, :], lhsT=v_[:, kt, :], rhs=p_[:], start=(kt == 0), stop=(kt == NKT - 1))
                ins = k.e["pe"].matmul(lp_[:, :], lhsT=ones_1[:, :], rhs=p_[:], start=(kt == 0), stop=(kt == NKT - 1))
                ev = k.mark("pe", ins)
                k.done(ev, [p_, v_, ones_1], [])
                if kt == NKT - 1:
                    op_.w = ev
                    lp_.w = ev
                    op_.r = []
                    lp_.r = []
                    k.op("dve", lambda e: e.reciprocal(out=rl_[:], in_=lp_[:]), reads=[lp_], writes=[rl_])
                    k.op("dve", lambda e: e.tensor_tensor(out=aT[:, h, qc * 512:(qc + 1) * 512], in0=op_[:], in1=rl_[:], op=ALU.mult),
                         reads=[op_, rl_], writes=[aT])

            for i in range(len(steps) + SKEW):
                if i < len(steps):
                    emit_S(i)
                if i >= SKEW:
                    emit_PV(i - SKEW)
        proj_residual(k, C, aT, BH, w_ob, out, out)


def t5_bucket_np(dist):
    dist = np.asarray(dist, dtype=np.int64)
    NB, ME, MD = 32, 16, 2048
    d_f = np.maximum(dist, 1).astype(np.float32)
    large = ME + (np.log(d_f / np.float32(ME)) / np.float32(math.log(MD / ME)) * np.float32(NB - ME)).astype(np.int32)
    large = np.minimum(large, NB - 1)
    return np.where(dist < ME, dist, large).astype(np.int64)


def prep_inputs(C, inp):
    D, T, AH, BH, NFB, RPB, QR, KVR = C.D, C.T, C.AH, C.BH, C.NFB, C.RPB, C.QR, C.KVR
    f32 = np.float32
    A = {kk: np.asarray(v) for kk, v in inp.items()}
    shared = {}
    shared["a_w_qkv"] = np.ascontiguousarray(A["a_w_qkv"][0])
    shared["a_w_o"] = np.ascontiguousarray(A["a_w_o"][0])
    shared["kv_w_down"] = np.ascontiguousarray(A["kv_w_down"])
    shared["kv_w_up"] = np.ascontiguousarray(A["kv_w_up"])
    shared["b_w_dq"] = np.ascontiguousarray(A["b_w_dq"][0])
    shared["b_w_uq"] = np.ascontiguousarray(A["b_w_uq"][0])
    shared["b_w_o"] = np.ascontiguousarray(A["b_w_o"][0])
    for L in (0, 1):
        shared["ffn_w_up%d" % L] = np.ascontiguousarray(A["ffn_w_up"][L])
        shared["ffn_w_down%d" % L] = np.ascontiguousarray(A["ffn_w_down"][L])
    rows = [A["a_attn_norm"][0], A["kv_norm"], A["b_attn_norm"][0], A["ffn_norm"][0], A["ffn_norm"][1]]
    shared["gbc"] = np.ascontiguousarray(np.stack([np.broadcast_to(r[None, :], (128, D)) for r in rows]).astype(f32))
    cols = [A["a_q_norm"][0, g] for g in range(3)] + [A["a_k_norm"][0, g] for g in range(3)]
    cols += [A["b_q_latent_norm"][0][c * 128:(c + 1) * 128] for c in range(QR // 128)]
    cols += [A["kv_latent_norm"][c * 128:(c + 1) * 128] for c in range(KVR // 128)]
    cols += [A["b_q_norm"][0][:128], A["kv_k_norm"][:128]]
    shared["gcol"] = np.ascontiguousarray(np.stack(cols, axis=1).astype(f32))
    qg, kg = A["b_q_norm"][0][128:192], A["kv_k_norm"][128:192]
    sw = lambda v: np.concatenate([v[32:], v[:32]])
    shared["grope"] = np.ascontiguousarray(np.stack([qg, sw(qg), kg, sw(kg)], axis=1).astype(f32))
    cw = A["ffn_conv_w"]
    shared["convw"] = np.ascontiguousarray(cw.reshape(2, 3, 2 * NFB, 128).transpose(3, 0, 1, 2).astype(f32))
    shared["convb"] = np.ascontiguousarray(A["ffn_conv_b"].reshape(2, 2 * NFB, 128).transpose(2, 0, 1).astype(f32))
    a = np.arange(128)[:, None]
    q = np.arange(128)[None, :]
    tabs = np.empty((3 * AH, 128, 256), f32)
    rel = A["rel_bias"]
    for g, (_, d) in enumerate(C.GROUPS):
        stA = q - a + 128
        stB = q - a
        bA = t5_bucket_np(np.maximum(stA, 0) * d)
        bB = t5_bucket_np(np.maximum(stB, 0) * d)
        for h in range(AH):
            tabs[g * AH + h, :, 0:128] = np.where(a >= q, rel[bA, g, h], f32(NEG))
            tabs[g * AH + h, :, 128:256] = np.where(a <= q, rel[bB, g, h], f32(NEG))
    shared["t5tab"] = tabs
    shared["ident"] = np.eye(128, dtype=f32)
    x = A["x"]
    in_maps = []
    for c in range(C.NCORE):
        b, jb = c // RPB, c % RPB
        s0 = jb * T
        m = dict(shared)
        m["x"] = np.ascontiguousarray(x[b, s0:s0 + T])
        xh = np.zeros((2 * T, D), f32)
        lo = s0 - 2 * T
        if lo >= 0:
            xh[:] = x[b, lo:s0]
        elif s0 > 0:
            xh[2 * T - s0:] = x[b, 0:s0]
        m["xh"] = xh
        hvv = np.full((128, 3), 0.0 if jb > 0 else NEG, f32)
        for g_, H_ in enumerate(C.HALO):
            d_ = C.GROUPS[g_][1]
            relpos = -(128 - np.arange(128)) * d_
            hvv[:, g_] = np.where(s0 + relpos >= 0, 0.0, NEG)
        m["hv"] = hvv
        sel = np.zeros((2, C.NCORE), f32)
        if jb > 0:
            sel[:, c - 1] = 1.0
        m["sel"] = sel
        pos = np.arange(s0, s0 + T, dtype=f32)
        inv = (f32(10000.0) ** (-np.arange(0, 64, 2, dtype=f32) / f32(64))).astype(f32)
        ang = (pos[None, :] * inv[:, None]).astype(f32)
        cs, sn = np.cos(ang).astype(f32), np.sin(ang).astype(f32)
        m["cos2"] = np.ascontiguousarray(np.concatenate([cs, cs], axis=0))
        m["sin2"] = np.ascontiguousarray(np.concatenate([-sn, sn], axis=0))
        rbL = np.zeros((128, RPB), f32)
        rbG = np.zeros((128, RPB), f32)
        tmix = np.zeros((128, RPB, 4, 512), f32)
        kk = np.arange(128)[:, None]
        qq = np.arange(512)[None, :]
        for kr in range(RPB):
            rbL[:, kr] = 0.0 if kr <= jb else NEG
            rbG[:, kr] = 0.0 if kr < jb else NEG
            for rel_ in range(4):
                if kr < jb:
                    tmix[:, kr, rel_] = 0.0
                elif kr > jb:
                    tmix[:, kr, rel_] = NEG
                else:
                    tmix[:, kr, rel_] = np.where(rel_ * 128 + kk <= qq, 0.0, NEG)
        m["rbL"], m["rbG"] = rbL, rbG
        m["tmix"] = np.ascontiguousarray(tmix.reshape(128, -1))
        in_maps.append(m)
    return in_maps


def input_names(nc):
    names = []
    for alloc in nc.allocations:
        if isinstance(alloc, mybir.MemoryLocationSet) and alloc.kind == "ExternalInput":
            names.append(alloc.memorylocations[0].name)
    return names


def launch(C, seg, in_maps, trace=False):
    nc = build(C, seg)
    names = set(input_names(nc))
    maps = [{kk: v for kk, v in m.items() if kk in names} for m in in_maps]
    missing = names - set(maps[0].keys())
    missing = {n for n in missing if "partition" not in n}
    assert not missing, missing
    res = run_bass_kernel_spmd(nc, maps, core_ids=list(range(C.NCORE)), trace=trace)
    return res


def run(C, inputs, trace=False, fused=False, upto=4):
    in_maps = prep_inputs(C, inputs)
    NCR, T, RPB = C.NCORE, C.T, C.RPB
    if fused:
        res = launch(C, 0, in_maps, trace)
        outs = [res.results[c]["out"] for c in range(NCR)]
        return np.stack(outs).reshape(C.BATCH, C.SEQ, C.D).astype(np.float32), res

    def halo_rows(hs):
        return np.ascontiguousarray(np.concatenate([h[T - 2:T] for h in hs], axis=0))

    res = launch(C, 1, in_maps, trace)
    hs = [res.results[c]["out"] for c in range(NCR)]
    if upto >= 2:
        hg = halo_rows(hs)
        for c in range(NCR):
            in_maps[c]["hin"] = hs[c]
            in_maps[c]["hal_g"] = hg
        res = launch(C, 2, in_maps, trace)
        hs = [res.results[c]["out"] for c in range(NCR)]
    if upto >= 3:
        for b in range(C.BATCH):
            rk = range(b * RPB, (b + 1) * RPB)
            kn = np.ascontiguousarray(np.concatenate([res.results[r]["KBn_l"] for r in rk], axis=0))
            kr_ = np.ascontiguousarray(np.concatenate([res.results[r]["KBr_l"] for r in rk], axis=0))
            vv = np.ascontiguousarray(np.concatenate([res.results[r]["VB_l"] for r in rk], axis=0))
            for c in rk:
                in_maps[c]["KBn_g"], in_maps[c]["KBr_g"], in_maps[c]["VB_g"] = kn, kr_, vv
        for c in range(NCR):
            in_maps[c]["hin"] = hs[c]
        res = launch(C, 3, in_maps, trace)
        hs = [res.results[c]["out"] for c in range(NCR)]
    if upto >= 4:
        hg = halo_rows(hs)
        for c in range(NCR):
            in_maps[c]["hin"] = hs[c]
            in_maps[c]["hal_g"] = hg
        res = launch(C, 4, in_maps, trace)
        hs = [res.results[c]["out"] for c in range(NCR)]
    y = np.stack(hs).reshape(C.BATCH, C.SEQ, C.D).astype(np.float32)
    return y, res


def kernel(**inputs):
    y, _ = run(FULL, inputs)
    return y
```
